# Optimizing a Trainium2 kernel written in Bass

```python
import math
import jax
import jax.numpy as jnp
from jax import lax
import numpy as np

D_MODEL = 2048
BATCH = 1
SEQ = 16384
DEPTH = 1

MIX_WIDTH = D_MODEL
ML_HEADS = 4
ML_V_W = MIX_WIDTH // 2
ML_DV = ML_V_W // ML_HEADS
ML_DQK = ML_DV // 2
ML_QK_W = ML_HEADS * ML_DQK
ML_CHUNK = 64
CONV_K = 4
MB_W = MIX_WIDTH - ML_V_W
MB_HEADS = 8
MB_DH = MB_W // MB_HEADS
MB_BLOCK = 256
MB_TOPK = 3
MB_QCHUNK = 64
ROPE_THETA = 500000.0
ROPE_DIM = MB_DH // 4
D_FF = ((8 * D_MODEL + 3 * 256 - 1) // (3 * 256)) * 256
LN_EPS = 1e-5
DN_ALPHA = (2 * DEPTH) ** 0.25
DN_BETA = (8 * DEPTH) ** -0.25
IN_SIZES = (MB_W, MB_W, MB_W, ML_QK_W, ML_QK_W, ML_V_W, ML_V_W, ML_HEADS, ML_HEADS)
IN_OFFSETS = tuple(int(o) for o in np.cumsum(IN_SIZES)[:-1])
IN_WIDTH = int(sum(IN_SIZES))

kernel_name = "hymba_mlstm_moba_deepnorm_adaln"


def layer_norm(x, w, b):
    xf = x.astype(jnp.float32)
    mu = jnp.mean(xf, axis=-1, keepdims=True)
    var = jnp.mean(jnp.square(xf - mu), axis=-1, keepdims=True)
    return ((xf - mu) * lax.rsqrt(var + LN_EPS) * w + b).astype(x.dtype)


def rope_partial(x, pos):
    half = ROPE_DIM // 2
    inv = ROPE_THETA ** (-jnp.arange(half, dtype=jnp.float32) * 2.0 / ROPE_DIM)
    ang = pos.astype(jnp.float32)[:, None] * inv[None, :]
    cos = jnp.cos(ang)[None, :, None, :]
    sin = jnp.sin(ang)[None, :, None, :]
    xf = x[..., :ROPE_DIM].astype(jnp.float32)
    x1, x2 = xf[..., :half], xf[..., half:]
    rot = jnp.concatenate([x1 * cos - x2 * sin, x2 * cos + x1 * sin], axis=-1)
    return jnp.concatenate([rot.astype(x.dtype), x[..., ROPE_DIM:]], axis=-1)


def causal_conv(x, w, b):
    ch = x.shape[-1]
    y = lax.conv_general_dilated(x, w[:, None, :].astype(x.dtype), (1,), [(CONV_K - 1, 0)],
                                 dimension_numbers=('NWC', 'WIO', 'NWC'), feature_group_count=ch)
    return y + b.astype(x.dtype)


def moba_attention(q, k, v):
    B, S, H, Dh = q.shape
    nb = -(-S // MB_BLOCK)
    pad = nb * MB_BLOCK - S
    kp = jnp.pad(k, ((0, 0), (0, pad), (0, 0), (0, 0)))
    vp = jnp.pad(v, ((0, 0), (0, pad), (0, 0), (0, 0)))
    kblk = kp.reshape(B, nb, MB_BLOCK, H, Dh).transpose(0, 3, 1, 2, 4)
    vblk = vp.reshape(B, nb, MB_BLOCK, H, Dh).transpose(0, 3, 1, 2, 4)
    kmean = jnp.mean(kblk.astype(jnp.float32), axis=3)
    topk = min(MB_TOPK, nb)
    nq = S // MB_QCHUNK
    q_chunks = q.reshape(B, nq, MB_QCHUNK, H, Dh).transpose(1, 0, 2, 3, 4)
    scale = Dh ** -0.5
    b_ix = jnp.arange(B)[:, None, None, None]
    h_ix = jnp.arange(H)[None, None, :, None]

    def one_chunk(args):
        qc, ci = args
        q0 = ci * MB_QCHUNK
        blk = q0 // MB_BLOCK
        qpos = q0 + jnp.arange(MB_QCHUNK)
        gate = jnp.einsum('bqhd,bhnd->bqhn', qc.astype(jnp.float32), kmean)
        gate = jnp.where(jnp.arange(nb) < blk, gate, -jnp.inf)
        _, idx = lax.top_k(gate, topk)
        valid = jnp.arange(topk) < blk
        ksel = kblk[b_ix, h_ix, idx]
        vsel = vblk[b_ix, h_ix, idx]
        s_sel = jnp.einsum('bqhd,bqhjkd->bqhjk', qc, ksel).astype(jnp.float32) * scale
        s_sel = jnp.where(valid[:, None], s_sel, -jnp.inf)
        kown = lax.dynamic_index_in_dim(kblk, blk, axis=2, keepdims=False)
        vown = lax.dynamic_index_in_dim(vblk, blk, axis=2, keepdims=False)
        s_own = jnp.einsum('bqhd,bhkd->bqhk', qc, kown).astype(jnp.float32) * scale
        kpos = blk * MB_BLOCK + jnp.arange(MB_BLOCK)
        causal = (kpos[None, :] <= qpos[:, None])[None, :, None, :]
        s_own = jnp.where(causal, s_own, -jnp.inf)
        logits = jnp.concatenate([s_sel.reshape(B, MB_QCHUNK, H, topk * MB_BLOCK), s_own], axis=-1)
        p = jax.nn.softmax(logits, axis=-1)
        p_sel = p[..., :topk * MB_BLOCK].reshape(B, MB_QCHUNK, H, topk, MB_BLOCK).astype(v.dtype)
        p_own = p[..., topk * MB_BLOCK:].astype(v.dtype)
        return (jnp.einsum('bqhjk,bqhjkd->bqhd', p_sel, vsel)
                + jnp.einsum('bqhk,bhkd->bqhd', p_own, vown))

    out = lax.map(one_chunk, (q_chunks, jnp.arange(nq)))
    return out.transpose(1, 0, 2, 3, 4).reshape(B, S, H * Dh)


def _mlstm_chunk_step(carry, inp):
    c_st, n_st, m_st = carry
    q, k, v, ig, lf = inp
    L = q.shape[-2]
    causal = jnp.tril(jnp.ones((L, L), dtype=bool))
    b = jnp.cumsum(lf, axis=-1)
    a = b + m_st[..., None]
    d = jnp.where(causal, b[..., :, None] - b[..., None, :] + ig[..., None, :], -jnp.inf)
    m_t = jnp.maximum(a, jnp.max(d, axis=-1))
    s = jnp.einsum('bhtd,bhsd->bhts', q, k) * jnp.exp(d - m_t[..., None])
    w_inter = jnp.exp(a - m_t)
    num = w_inter[..., None] * jnp.einsum('bhtd,bhdv->bhtv', q, c_st) + jnp.einsum('bhts,bhsv->bhtv', s, v)
    den = w_inter * jnp.einsum('bhtd,bhd->bht', q, n_st) + jnp.sum(s, axis=-1)
    h = num / jnp.maximum(jnp.abs(den), jnp.exp(-m_t))[..., None]
    b_last = b[..., -1]
    g = b_last[..., None] - b + ig
    m_new = jnp.maximum(b_last + m_st, jnp.max(g, axis=-1))
    w_k = jnp.exp(g - m_new[..., None])
    decay = jnp.exp(b_last + m_st - m_new)
    c_new = decay[..., None, None] * c_st + jnp.einsum('bhs,bhsd,bhsv->bhdv', w_k, k, v)
    n_new = decay[..., None] * n_st + jnp.einsum('bhs,bhsd->bhd', w_k, k)
    return (c_new, n_new, m_new), h


def mlstm(q, k, v, ig, fg):
    B, S, H, _ = q.shape
    nc = S // ML_CHUNK

    def to_chunks(t):
        t = t.astype(jnp.float32).reshape((B, nc, ML_CHUNK, H) + t.shape[3:])
        return jnp.moveaxis(t, (1, 3), (0, 2))

    lf = jax.nn.log_sigmoid(fg.astype(jnp.float32))
    init = (jnp.zeros((B, H, ML_DQK, ML_DV), jnp.float32),
            jnp.zeros((B, H, ML_DQK), jnp.float32),
            jnp.zeros((B, H), jnp.float32))
    _, h = lax.scan(_mlstm_chunk_step, init,
                    (to_chunks(q), to_chunks(k), to_chunks(v), to_chunks(ig), to_chunks(lf)))
    h = jnp.moveaxis(h, (0, 2), (1, 3))
    return h.reshape(B, S, H, ML_DV)


def setup_inputs(seed: int = 0) -> dict:
    key = jax.random.key(seed)
    ks = jax.random.split(key, 20)
    f32 = jnp.float32
    nrm = lambda k, shape, s: jax.random.normal(k, shape, f32) * s
    col_scale = np.ones((IN_WIDTH,), np.float32)
    col_scale[2 * MB_W:3 * MB_W] = DN_BETA
    v0 = 3 * MB_W + 2 * ML_QK_W
    col_scale[v0:v0 + ML_V_W] = DN_BETA
    return {
        "x": nrm(ks[0], (BATCH, SEQ, D_MODEL), 1.0),
        "c": nrm(ks[1], (BATCH, D_MODEL), 1.0),
        "w_ada": nrm(ks[2], (DEPTH, D_MODEL, 6 * D_MODEL), 0.5 * D_MODEL ** -0.5),
        "b_ada": nrm(ks[3], (DEPTH, 6 * D_MODEL), 0.02),
        "w_in": nrm(ks[4], (DEPTH, D_MODEL, IN_WIDTH), D_MODEL ** -0.5) * jnp.asarray(col_scale),
        "conv_w": nrm(ks[5], (DEPTH, CONV_K, 2 * ML_QK_W), CONV_K ** -0.5),
        "conv_b": nrm(ks[6], (DEPTH, 2 * ML_QK_W), 0.02),
        "ml_igate_b": nrm(ks[7], (DEPTH, ML_HEADS), 0.1),
        "ml_fgate_b": jnp.linspace(3.0, 6.0, ML_HEADS, dtype=f32)[None, :] + nrm(ks[8], (DEPTH, ML_HEADS), 0.01),
        "ml_norm_w": 1.0 + nrm(ks[9], (DEPTH, ML_V_W), 0.02),
        "w_out": nrm(ks[10], (DEPTH, MIX_WIDTH, D_MODEL), DN_BETA * MIX_WIDTH ** -0.5),
        "ln1_w": 1.0 + nrm(ks[11], (DEPTH, D_MODEL), 0.02),
        "ln1_b": nrm(ks[12], (DEPTH, D_MODEL), 0.02),
        "w_gu": nrm(ks[13], (DEPTH, D_MODEL, 2 * D_FF), DN_BETA * D_MODEL ** -0.5),
        "w_down": nrm(ks[14], (DEPTH, D_FF, D_MODEL), DN_BETA * D_FF ** -0.5),
        "ln2_w": 1.0 + nrm(ks[15], (DEPTH, D_MODEL), 0.02),
        "ln2_b": nrm(ks[16], (DEPTH, D_MODEL), 0.02),
    }


def reference(x, c, w_ada, b_ada, w_in, conv_w, conv_b, ml_igate_b, ml_fgate_b, ml_norm_w,
              w_out, ln1_w, ln1_b, w_gu, w_down, ln2_w, ln2_b):
    B, S, D = x.shape
    pos = jnp.arange(S)
    for l in range(DEPTH):
        ada = jax.nn.silu(c) @ w_ada[l] + b_ada[l]
        shift1, scale1, gate1, shift2, scale2, gate2 = [a[:, None, :] for a in jnp.split(ada, 6, axis=-1)]

        u = x * (1.0 + scale1) + shift1
        proj = u @ w_in[l]
        aq, ak, av, mq, mk, mv, mo, mi, mf = jnp.split(proj, IN_OFFSETS, axis=-1)

        aq = rope_partial(aq.reshape(B, S, MB_HEADS, MB_DH), pos)
        ak = rope_partial(ak.reshape(B, S, MB_HEADS, MB_DH), pos)
        a_out = moba_attention(aq, ak, av.reshape(B, S, MB_HEADS, MB_DH))

        qk = jax.nn.silu(causal_conv(jnp.concatenate([mq, mk], axis=-1), conv_w[l], conv_b[l]))
        mq, mk = jnp.split(qk, 2, axis=-1)
        h = mlstm(mq.reshape(B, S, ML_HEADS, ML_DQK),
                  mk.reshape(B, S, ML_HEADS, ML_DQK) * (ML_DQK ** -0.5),
                  mv.reshape(B, S, ML_HEADS, ML_DV),
                  mi + ml_igate_b[l], mf + ml_fgate_b[l])
        mu = jnp.mean(h, axis=-1, keepdims=True)
        var = jnp.mean(jnp.square(h - mu), axis=-1, keepdims=True)
        h = ((h - mu) * lax.rsqrt(var + LN_EPS)).reshape(B, S, ML_V_W) * ml_norm_w[l]
        m_out = (h * jax.nn.sigmoid(mo.astype(jnp.float32))).astype(x.dtype)

        mix = jnp.concatenate([m_out, a_out.astype(x.dtype)], axis=-1) @ w_out[l]
        x = layer_norm(DN_ALPHA * x + (1.0 + gate1) * mix, ln1_w[l], ln1_b[l])

        u = x * (1.0 + scale2) + shift2
        g, up = jnp.split(u @ w_gu[l], 2, axis=-1)
        f = (jax.nn.silu(g) * up) @ w_down[l]
        x = layer_norm(DN_ALPHA * x + (1.0 + gate2) * f, ln2_w[l], ln2_b[l])
    return x
```

```python
import math
from contextlib import ExitStack

import numpy as np
import ml_dtypes

import concourse.bass as bass
import concourse.mybir as mybir
from concourse.bass_utils import run_bass_kernel_spmd

F32 = mybir.dt.float32
BF16 = mybir.dt.bfloat16
I32 = mybir.dt.int32
AF = mybir.ActivationFunctionType
ALU = mybir.AluOpType
AX = mybir.AxisListType

NCORE = 8
D = 2048
KD = 16
DFF = 5632
KF = 44
KFP = 48
NCOL = 898
LN_EPS = 1e-5
DEPTH = 1
DN_ALPHA = (2 * DEPTH) ** 0.25
NEG = -30000.0
ROPE_THETA = 500000.0


class Sched:
    NSLOT = 6
    ENG = {"pe": "tensor", "act": "scalar", "dve": "vector", "pool": "gpsimd", "sp": "sync"}

    def __init__(self, nc, es):
        self.nc = nc
        self.ops = []
        self.esem = {e: es.enter_context(nc.semaphore("s_" + e)) for e in self.ENG}
        self.dq = {e: [es.enter_context(nc.semaphore("d_%s%d" % (e, s))) for s in range(self.NSLOT)]
                   for e in ("sp", "pool", "act")}
        self.ccs = [es.enter_context(nc.semaphore("cc%d" % s)) for s in range(8)]
        self.cci = 0
        self.ecount = {e: 0 for e in self.ENG}
        self.dcount = {e: [0] * self.NSLOT for e in self.dq}
        self.dlast = {e: [None] * self.NSLOT for e in self.dq}
        self.drr = {e: 0 for e in self.dq}
        self.waited = {e: {} for e in self.ENG}

    def op(self, eng, meth, *args, r=(), w=(), **kw):
        bw = tuple(k for k in r if len(k) == 2 and k[0] == "b" and k[1].isdigit() and k not in w)
        self.ops.append(dict(eng=eng, meth=meth, args=args, kw=kw, r=tuple(r), w=tuple(w) + bw, kind="c"))

    def dma(self, eng, out, in_, r=(), w=()):
        self.ops.append(dict(eng=eng, meth="dma_start", args=(), kw=dict(out=out, in_=in_),
                             r=tuple(r), w=tuple(w), kind="d"))

    def cc(self, src, dst, r=(), w=()):
        self.ops.append(dict(eng="pool", meth="collective_compute",
                             args=("AllGather", ALU.bypass),
                             kw=dict(replica_groups=[list(range(NCORE))], ins=[src], outs=[dst]),
                             r=tuple(r), w=tuple(w), kind="cc"))

    def _wait(self, ename, ev):
        sem, val, _ = ev
        key = id(sem)
        if self.waited[ename].get(key, 0) >= val:
            return
        getattr(self.nc, self.ENG[ename]).wait_ge(sem, val)
        self.waited[ename][key] = val

    def flush(self):
        nc = self.nc
        ops = self.ops
        self.ops = []
        last_w = {}
        readers = {}
        for i, o in enumerate(ops):
            deps = set()
            for k in o["r"]:
                if k in last_w:
                    deps.add(last_w[k])
            for k in o["w"]:
                if k in last_w:
                    deps.add(last_w[k])
                deps.update(readers.get(k, ()))
            deps.discard(i)
            for k in o["r"]:
                readers.setdefault(k, []).append(i)
            for k in o["w"]:
                last_w[k] = i
                readers[k] = []
            o["deps"] = {d for d in deps
                         if not (ops[d]["eng"] == "pe" and o["eng"] == "pe"
                                 and ops[d]["kind"] == "c" and o["kind"] == "c")}
        signaled = set()
        for o in ops:
            signaled.update(o["deps"])
        lastc = {}
        for i, o in enumerate(ops):
            if o["kind"] == "c":
                lastc[o["eng"]] = i
        signaled.update(lastc.values())
        finals = []
        for i, o in enumerate(ops):
            e = o["eng"]
            if o["kind"] == "c":
                if i in signaled:
                    self.ecount[e] += 1
                    o["ev"] = (self.esem[e], self.ecount[e], 1)
                else:
                    o["ev"] = None
                o["prev"] = None
            elif o["kind"] == "d":
                s = self.drr[e]
                self.drr[e] = (s + 1) % self.NSLOT
                o["prev"] = self.dlast[e][s]
                self.dcount[e][s] += 16
                o["ev"] = (self.dq[e][s], self.dcount[e][s], 16)
                self.dlast[e][s] = o["ev"]
            else:
                o["ev"] = (self.ccs[self.cci], 1, 1)
                self.cci += 1
                o["prev"] = None
                finals.append(o["ev"])
        for i, o in enumerate(ops):
            e = o["eng"]
            for d in sorted(o["deps"]):
                self._wait(e, ops[d]["ev"])
            if o["prev"] is not None:
                self._wait(e, o["prev"])
            ins = getattr(getattr(nc, self.ENG[e]), o["meth"])(*o["args"], **o["kw"])
            if o["ev"] is not None:
                ins.then_inc(o["ev"][0], o["ev"][2])
        for e, i in lastc.items():
            self._wait(e, ops[i]["ev"])
        for e in self.dq:
            for s in range(self.NSLOT):
                if self.dlast[e][s] is not None:
                    self._wait("sp", self.dlast[e][s])
        for ev in finals:
            self._wait("sp", ev)
        for e, i in lastc.items():
            self._wait("sp", ops[i]["ev"])
        nc.all_engine_barrier()


def build(S, stop=9, cut=9):
    TPC = S // NCORE
    NT128 = S // 128
    NG = S // 512
    GB = min(512, TPC)
    GT = GB // 128
    NGB = TPC // GB
    assert S % 512 == 0 and TPC % 128 == 0 and TPC % GB == 0

    nc = bass.Bass("TRN2", target_bir_lowering=False)
    di = lambda n, sh, dt: nc.dram_tensor(n, sh, dt, kind="ExternalInput")
    xs = di("xs", [TPC, D], F32)
    cvec = di("cvec", [16, 128], F32)
    wada = di("wada", [D, 1536], F32)
    bada = di("bada", [1, 1536], F32)
    win = di("win", [D, NCOL], F32)
    convw = di("convw", [128, 8], F32)
    convb = di("convb", [128, 2], F32)
    gateb = di("gateb", [128, 2], F32)
    wout_s = di("wout_s", [256, D], F32)
    wgu_s = di("wgu_s", [256, 2 * DFF], F32)
    wdn_s = di("wdn_s", [768, D], F32)
    ln1w = di("ln1w", [1, D], F32)
    ln1b = di("ln1b", [1, D], F32)
    ln2w = di("ln2w", [1, D], F32)
    ln2b = di("ln2b", [1, D], F32)
    mlnw = di("mlnw", [1, 1024], F32)
    c_identb = di("c_identb", [128, 128], BF16)
    c_identf = di("c_identf", [128, 128], F32)
    c_tri = di("c_tri", [128, 128], F32)
    c_ones = di("c_ones", [128, 128], F32)
    c_trineg = di("c_trineg", [128, 128], BF16)
    c_esel = di("c_esel", [64, 64 * 128], BF16)
    c_ropep = di("c_ropep", [128, 32], F32)
    c_cos = di("c_cos", [32, S], F32)
    c_sin = di("c_sin", [32, S], F32)
    cidx = di("cidx", [1, 1], I32)
    y = nc.dram_tensor("y", [TPC, D], F32, kind="ExternalOutput")

    ada_b = nc.dram_tensor("ada_b", [1, 1536], F32)
    ada_all = nc.dram_tensor("ada_all", [8, 1536], F32)
    x_b = nc.dram_tensor("x_b", [TPC, D], BF16)
    x_all = nc.dram_tensor("x_all", [S, D], BF16)
    wout_b = nc.dram_tensor("wout_b", [256, D], BF16)
    wout_all = nc.dram_tensor("wout_all", [D, D], BF16)
    wgu_b = nc.dram_tensor("wgu_b", [256, 2 * DFF], BF16)
    wgu_all = nc.dram_tensor("wgu_all", [D, 2 * DFF], BF16)
    wdn_b = nc.dram_tensor("wdn_b", [768, D], BF16)
    wdn_all = nc.dram_tensor("wdn_all", [KFP * 128, D], BF16)
    ex_b = nc.dram_tensor("ex_b", [S, 384], BF16)
    ex_all = nc.dram_tensor("ex_all", [8 * S, 384], BF16)

    ISQ = 1.0 / math.sqrt(128.0)
    LNC = math.log(ISQ)

    with ExitStack() as top:
        sch = Sched(nc, top)
        op, dma, cc = sch.op, sch.dma, sch.cc
        rcid = top.enter_context(nc.sync.register("rcid"))
        nc.sync.reg_load(rcid, cidx[0:1, 0:1])
        off = nc.sync.snap(rcid, min_val=0, max_val=NCORE - 1)

        sb = lambda n, sh, dt: top.enter_context(nc.sbuf_tensor(n, sh, dt))
        banks = [top.enter_context(nc.psum_tensor("bank%d" % i, [128, 512], F32)) for i in range(8)]
        bkb = [bk_[:, :].bitcast(BF16) for bk_ in banks]
        bk = ["b%d" % i for i in range(8)]
        identb = sb("identb", [128, 128], BF16)
        identf = sb("identf", [128, 128], F32)
        tri = sb("tri", [128, 128], F32)
        onesf = sb("onesf", [128, 128], F32)
        adaT = sb("adaT", [128, 96], F32)
        sc1p = sb("sc1p", [128, 16], F32)
        sc2p = sb("sc2p", [128, 16], F32)
        lnc_t = sb("lnc_t", [128, 1], F32)
        one_t = sb("one_t", [128, 1], F32)

        for t_, s_, n_ in ((identb, c_identb, "identb"), (identf, c_identf, "identf"), (tri, c_tri, "tri"),
                           (onesf, c_ones, "onesf")):
            dma("sp", t_[:, :], s_[:, :], w=[n_])
        op("pool", "memset", lnc_t[:, :], LNC, w=["lnc_t"])
        op("pool", "memset", one_t[:, :], 1.0, w=["one_t"])
        eps_t = sb("eps_t", [128, 1], F32)
        op("pool", "memset", eps_t[:, :], LN_EPS, w=["eps_t"])

        with ExitStack() as p0:
            sb0 = lambda n, sh, dt: p0.enter_context(nc.sbuf_tensor(n, sh, dt))
            wa = sb0("wa", [128, 16, 1536], F32)
            c16 = sb0("c16", [16, 128], F32)
            cT = sb0("cT", [128, 16], F32)
            scT = sb0("scT", [128, 16], F32)
            badat = sb0("badat", [1, 1536], F32)
            adas = sb0("adas", [1, 1536], F32)
            ada96 = sb0("ada96", [96, 128], F32)
            g1b = sb0("g1b", [128, D], F32)
            g2b = sb0("g2b", [128, D], F32)
            wo32 = sb0("wo32", [128, 2, D], F32)
            wo16 = sb0("wo16", [128, 2, D], BF16)
            wd32 = sb0("wd32", [128, 2, D], F32)
            wd16 = sb0("wd16", [128, 6, D], BF16)

            dma("pool", x_b[:, :], xs[:, :], w=["x_b"])
            dma("sp", wa[:, :, :], wada.ap().rearrange("(k p) c -> p k c", p=128), w=["wa"])
            dma("sp", c16[:, :], cvec[:, :], w=["c16"])
            dma("sp", badat[:, :], bada[:, :], w=["badat"])
            op("pe", "transpose", banks[0][:, 0:16], c16[:, :], identf[0:16, 0:16], r=["c16", "identf"], w=[bk[0]])
            op("dve", "tensor_copy", cT[:, :], banks[0][:, 0:16], r=[bk[0]], w=["cT"])
            op("act", "activation", scT[:, :], cT[:, :], AF.Silu, r=["cT"], w=["scT"])
            for ct in range(3):
                for k in range(16):
                    op("pe", "matmul", banks[1 + ct][0:1, :], scT[:, k:k + 1], wa[:, k, 512 * ct:512 * ct + 512],
                       start=(k == 0), stop=(k == 15), r=["scT", "wa"], w=[bk[1 + ct]])
                op("dve", "tensor_tensor", adas[0:1, 512 * ct:512 * ct + 512], banks[1 + ct][0:1, :],
                   badat[0:1, 512 * ct:512 * ct + 512], ALU.add, r=[bk[1 + ct], "badat"], w=["adas"])
            dma("sp", ada_b[:, :], adas[:, :], r=["adas"], w=["ada_b"])
            cc(ada_b.ap(), ada_all.ap(), r=["ada_b"], w=["ada_all"])
            cc(x_b.ap(), x_all.ap(), r=["x_b"], w=["x_all"])
            dma("pool", wgu_b[:, :], wgu_s[:, :], w=["wgu_b"])
            cc(wgu_b.ap(), wgu_all.ap(), r=["wgu_b"], w=["wgu_all"])
            adaflat = ada_all.ap().rearrange("a b -> (a b)")
            dma("sp", ada96[:, :], adaflat.rearrange("(m p) -> m p", p=128), r=["ada_all"], w=["ada96"])
            op("pe", "transpose", banks[0][:, 0:96], ada96[:, :], identf[0:96, 0:96], r=["ada96", "identf"], w=[bk[0]])
            op("dve", "tensor_copy", adaT[:, :], banks[0][:, 0:96], r=[bk[0]], w=["adaT"])
            op("dve", "tensor_scalar_add", sc1p[:, :], adaT[:, 16:32], 1.0, r=["adaT"], w=["sc1p"])
            op("dve", "tensor_scalar_add", sc2p[:, :], adaT[:, 64:80], 1.0, r=["adaT"], w=["sc2p"])
            dma("sp", g1b[:, :], adaflat[4096:6144].partition_broadcast(128), r=["ada_all"], w=["g1b"])
            dma("sp", g2b[:, :], adaflat[10240:12288].partition_broadcast(128), r=["ada_all"], w=["g2b"])
            op("pool", "tensor_scalar_add", g1b[:, :], g1b[:, :], 1.0, r=["g1b"], w=["g1b"])
            op("pool", "tensor_scalar_add", g2b[:, :], g2b[:, :], 1.0, r=["g2b"], w=["g2b"])
            dma("sp", wo32[:, :, :], wout_s.ap().rearrange("(a p) c -> p a c", p=128), w=["wo32"])
            for a in range(2):
                op("dve", "tensor_tensor", wo16[:, a, :], wo32[:, a, :], g1b[:, :], ALU.mult,
                   r=["wo32", "g1b"], w=["wo16"])
            dma("sp", wout_b.ap().rearrange("(a p) c -> p a c", p=128), wo16[:, :, :], r=["wo16"], w=["wout_b"])
            cc(wout_b.ap(), wout_all.ap(), r=["wout_b"], w=["wout_all"])
            wdv = wdn_s.ap().rearrange("(a p) c -> p a c", p=128)
            for h3 in range(3):
                dma("sp", wd32[:, :, :], wdv[:, 2 * h3:2 * h3 + 2, :], w=["wd32"])
                for a in range(2):
                    op("dve", "tensor_tensor", wd16[:, 2 * h3 + a, :], wd32[:, a, :], g2b[:, :], ALU.mult,
                       r=["wd32", "g2b"], w=["wd16"])
            dma("sp", wdn_b.ap().rearrange("(a p) c -> p a c", p=128), wd16[:, :, :], r=["wd16"], w=["wdn_b"])
            cc(wdn_b.ap(), wdn_all.ap(), r=["wdn_b"], w=["wdn_all"])
            sch.flush()
        if stop == 0:
            return nc

        with ExitStack() as pa:
            sba = lambda n, sh, dt: pa.enter_context(nc.sbuf_tensor(n, sh, dt))
            winb = sba("winb", [128, 16, NCOL], BF16)
            kT_all = sba("kT_all", [128, S], BF16)
            V_all = sba("V_all", [128, NT128, 129], BF16)
            esel = sba("esel", [64, 64 * 128], BF16)
            trineg = sba("trineg", [128, 128], BF16)
            ropep = sba("ropep", [128, 32], F32)
            cw = sba("cw", [128, 8], F32)
            cbias = sba("cbias", [128, 2], F32)
            gb_t = sba("gb_t", [128, 2], F32)
            xin = [sba("xin%d" % i, [128, 4, D], BF16) for i in range(2)]
            uT = sba("uT", [128, 16, 512], BF16)
            cos_t = [sba("cos_t%d" % i, [32, 512], F32) for i in range(2)]
            sin_t = [sba("sin_t%d" % i, [32, 512], F32) for i in range(2)]
            tq = sba("tq", [128, 512], F32)
            tk = sba("tk", [128, 512], F32)
            r1 = sba("r1", [32, 512], F32)
            r2 = sba("r2", [32, 512], F32)
            qT_g = sba("qT_g", [128, 512], BF16)
            ksum = sba("ksum", [128, 64], F32)
            gt = sba("gt", [128, 64], F32)
            m8 = sba("m8", [128, 8], F32)
            negm = sba("negm", [128, 64], BF16)
            negmT = sba("negmT", [64, 512], BF16)
            cb = [sba("cb%d" % i, [128, 515], F32) for i in range(2)]
            acc = sba("acc", [128, 512], F32)
            mqk = [sba("mqk%d" % i, [128, 512], BF16) for i in range(2)]
            MV_g = sba("MV_g", [128, 4, 129], BF16)
            exs = [sba("exs%d" % i, [128, 4, 384], BF16) for i in range(2)]
            gts = sba("gts", [128, 4, 2], F32)
            lt = sba("lt", [128, 4], F32)
            et = sba("et", [128, 4], F32)
            cum = sba("cum", [128, 8], F32)
            a16 = sba("a16", [128, 16], F32)
            e16 = sba("e16", [128, 16], F32)
            SpT = sba("SpT", [128, 128], BF16)
            K2 = sba("K2", [128, 128], BF16)
            Cst = sba("Cst", [128, 129], F32)
            Cbf = sba("Cbf", [128, 129], BF16)
            den = sba("den", [128, 1], F32)
            rden = sba("rden", [128, 1], F32)
            PT = [sba("PT%d" % i, [128, 512], BF16) for i in range(3)]
            rinv = sba("rinv", [128, 4], F32)

            op("pool", "memset", Cst[:, :], 0.0, w=["Cst"])
            op("pool", "memset", Cbf[:, :], 0.0, w=["Cbf"])
            op("pool", "memset", ksum[:, :], 0.0, w=["ksum"])
            op("pool", "memset", V_all[:, :, 128:129], 1.0, w=["V_all"])
            op("pool", "memset", MV_g[:, :, 128:129], 1.0, w=["MV_g"])
            op("pool", "memset", cb[0][:, 0:3], 0.0, w=["cb0"])
            op("pool", "memset", cb[1][:, 0:3], 0.0, w=["cb1"])
            dma("pool", winb[:, :, :], win.ap().rearrange("(k p) c -> p k c", p=128), w=["winb"])
            dma("sp", esel[:, :], c_esel[:, :], w=["esel"])
            dma("sp", trineg[:, :], c_trineg[:, :], w=["trineg"])
            dma("sp", ropep[:, :], c_ropep[:, :], w=["ropep"])
            dma("sp", cw[:, :], convw[:, :], w=["cw"])
            dma("sp", cbias[:, :], convb[:, :], w=["cbias"])
            dma("sp", gb_t[:, :], gateb[:, :], w=["gb_t"])

            def load_x(j):
                b = j % 2
                dma("sp", xin[b][:, :, :], x_all[512 * j:512 * j + 512, :].rearrange("(a p) d -> p a d", p=128),
                    r=["x_all"], w=["xin%d" % b])
                dma("sp", cos_t[b][:, :], c_cos[:, 512 * j:512 * j + 512], w=["cos%d" % b])
                dma("sp", sin_t[b][:, :], c_sin[:, 512 * j:512 * j + 512], w=["sin%d" % b])

            def rope(tt, tname, b, psb):
                op("pe", "matmul", banks[psb][0:32, :], ropep[:, :], tt[:, :], start=True, stop=True,
                   r=[tname, "ropep"], w=[bk[psb]])
                op("dve", "tensor_tensor", r1[:, :], tt[0:32, :], cos_t[b][:, :], ALU.mult,
                   r=[tname, "cos%d" % b], w=["r1"])
                op("dve", "tensor_tensor", r2[:, :], banks[psb][0:32, :], sin_t[b][:, :], ALU.mult,
                   r=[bk[psb], "sin%d" % b], w=["r2"])
                op("dve", "tensor_tensor", tt[0:32, :], r1[:, :], r2[:, :], ALU.add, r=["r1", "r2"], w=[tname])

            load_x(0)
            for j in range(NG):
                b = j % 2
                if j + 1 < NG:
                    load_x(j + 1)
                ex = exs[b]
                exk = "exs%d" % b
                xk = "xin%d" % b
                for k in range(16):
                    pb = k % 2
                    pT = bkb[pb]
                    for a in range(4):
                        op("pe", "transpose", pT[:, 128 * a:128 * a + 128], xin[b][:, a, 128 * k:128 * k + 128],
                           identb[:, :], r=[xk, "identb"], w=[bk[pb]])
                    if k % 2 == 0:
                        op("act", "activation", uT[:, k, :], pT[:, 0:512], AF.Identity, bias=adaT[:, k:k + 1],
                           scale=sc1p[:, k:k + 1], r=[bk[pb], "adaT", "sc1p"], w=["uT"])
                    else:
                        op("dve", "tensor_scalar", uT[:, k, :], pT[:, 0:512], sc1p[:, k:k + 1], adaT[:, k:k + 1],
                           ALU.mult, ALU.add, r=[bk[pb], "adaT", "sc1p"], w=["uT"])
                if cut < 0.4:
                    continue
                for gi in (1, 0, 2, 3):
                    pf = 2 + (gi % 2)
                    if cut == 0.5 and gi != 1:
                        continue
                    if cut == 0.6 and gi not in (2, 3):
                        continue
                    if cut == 0.7 and gi not in (1, 0):
                        continue
                    for k in range(16):
                        op("pe", "matmul", banks[pf][:, :], winb[:, k, 128 * gi:128 * gi + 128], uT[:, k, :],
                           start=(k == 0), stop=(k == 15), r=["winb", "uT"], w=[bk[pf]])
                    if gi == 1:
                        op("act", "activation", tk[:, :], banks[pf][:, :], AF.Copy, r=[bk[pf]], w=["tk"])
                        rope(tk, "tk", b, 6)
                        for hh in range(2):
                            op("dve", "reduce_sum", ksum[:, 2 * j + hh:2 * j + hh + 1],
                               tk[:, 256 * hh:256 * hh + 256], AX.X, r=["tk"], w=["ksum"])
                        op("act", "activation", kT_all[:, 512 * j:512 * j + 512], tk[:, :], AF.Copy,
                           r=["tk"], w=["kT%d" % j])
                    elif gi == 0:
                        op("act", "activation", tq[:, :], banks[pf][:, :], AF.Copy, r=[bk[pf]], w=["tq"])
                        rope(tq, "tq", b, 6)
                        op("act", "activation", qT_g[:, :], tq[:, :], AF.Copy, r=["tq"], w=["qT_g"])
                        for a in range(4):
                            blk = 2 * j + a // 2
                            op("pe", "matmul", banks[6][:, 0:64], tq[:, 128 * a:128 * a + 128], ksum[:, :],
                               start=True, stop=True, r=["tq", "ksum"], w=[bk[6]])
                            op("dve", "tensor_copy", gt[:, :], banks[6][:, 0:64], r=[bk[6]], w=["gt"])
                            if blk < 64:
                                op("dve", "memset", gt[:, blk:64], -1e30, w=["gt"])
                            if blk >= 3:
                                op("dve", "max", m8[:, :], gt[:, :], r=["gt"], w=["m8"])
                                op("dve", "tensor_scalar", negm[:, :], gt[:, :], m8[:, 2:3], NEG, ALU.is_lt, ALU.mult,
                                   r=["gt", "m8"], w=["negm"])
                            else:
                                op("dve", "tensor_scalar", negm[:, :], gt[:, :], -1e29, NEG, ALU.is_lt, ALU.mult,
                                   r=["gt"], w=["negm"])
                            op("pe", "transpose", bkb[7][0:64, 0:128], negm[:, :], identb[:, :],
                               r=["negm", "identb"], w=[bk[7]])
                            op("dve", "tensor_copy", negmT[:, 128 * a:128 * a + 128], bkb[7][0:64, 0:128],
                               r=[bk[7]], w=["negmT"])
                    else:
                        w_ = gi - 2
                        ck = "cb%d" % w_
                        op("act", "activation", cb[w_][:, 3:515], banks[pf][:, :], AF.Copy, r=[bk[pf]], w=[ck])
                        op("dve", "tensor_scalar", acc[:, :], cb[w_][:, 0:512], cw[:, 4 * w_:4 * w_ + 1], None,
                           ALU.mult, r=[ck, "cw"], w=["acc"])
                        for t in range(1, 4):
                            op("dve", "scalar_tensor_tensor", acc[:, :], cb[w_][:, t:t + 512],
                               cw[:, 4 * w_ + t:4 * w_ + t + 1], acc[:, :], ALU.mult, ALU.add,
                               r=[ck, "cw", "acc"], w=["acc"])
                        op("act", "activation", mqk[w_][:, :], acc[:, :], AF.Silu, bias=cbias[:, w_:w_ + 1],
                           r=["acc", "cbias"], w=["mqk%d" % w_])
                        op("pool", "tensor_copy", cb[w_][:, 0:3], cb[w_][:, 512:515], r=[ck], w=[ck])
                if cut < 2:
                    continue
                for a in range(4):
                    pm = 4 + (a % 2)
                    for k in range(16):
                        op("pe", "matmul", banks[pm][:, 0:386], uT[:, k, 128 * a:128 * a + 128], winb[:, k, 512:898],
                           start=(k == 0), stop=(k == 15), r=["uT", "winb"], w=[bk[pm]])
                    op("act", "activation", V_all[:, 4 * j + a, 0:128], banks[pm][:, 0:128], AF.Copy,
                       r=[bk[pm]], w=["V%d" % j])
                    op("dve", "tensor_copy", MV_g[:, a, 0:128], banks[pm][:, 128:256], r=[bk[pm]], w=["MV_g"])
                    op("act", "activation", ex[:, a, 256:384], banks[pm][:, 256:384], AF.Sigmoid, r=[bk[pm]], w=[exk])
                    op("dve", "tensor_tensor", gts[:, a, :], banks[pm][:, 384:386], gb_t[:, :], ALU.add,
                       r=[bk[pm], "gb_t"], w=["gts"])
                if cut < 3:
                    continue
                op("act", "activation", et[:, :], gts[:, :, 1], AF.Exp, scale=-1.0, r=["gts"], w=["et"])
                op("act", "activation", lt[:, :], et[:, :], AF.Ln, bias=one_t[:, 0:1], r=["et", "one_t"], w=["lt"])
                op("pe", "matmul", banks[6][:, 0:4], tri[:, :], lt[:, :], start=True, stop=True,
                   r=["tri", "lt"], w=[bk[6]])
                op("pe", "matmul", banks[6][:, 4:8], onesf[:, :], lt[:, :], start=False, stop=True,
                   skip_group_check=True, r=["onesf", "lt"], w=[bk[6]])
                op("dve", "tensor_copy", cum[:, :], banks[6][:, 0:8], r=[bk[6]], w=["cum"])
                op("dve", "tensor_tensor", a16[:, 0:4], gts[:, :, 0], cum[:, 0:4], ALU.add, r=["gts", "cum"], w=["a16"])
                op("dve", "tensor_tensor", a16[:, 4:8], a16[:, 0:4], cum[:, 4:8], ALU.subtract,
                   r=["a16", "cum"], w=["a16"])
                op("dve", "tensor_copy", a16[:, 8:12], cum[:, 0:4], r=["cum"], w=["a16"])
                op("dve", "tensor_scalar", a16[:, 12:16], cum[:, 4:8], -1.0, None, ALU.mult, r=["cum"], w=["a16"])
                op("act", "activation", e16[:, 0:8], a16[:, 0:8], AF.Exp, bias=lnc_t[:, 0:1],
                   r=["a16", "lnc_t"], w=["e16"])
                op("act", "activation", e16[:, 8:16], a16[:, 8:16], AF.Exp, r=["a16"], w=["e16"])
                if cut < 4:
                    continue
                for a in range(4):
                    qs = mqk[0][:, 128 * a:128 * a + 128]
                    ks = mqk[1][:, 128 * a:128 * a + 128]
                    op("pe", "matmul", banks[6][:, 0:128], ks, qs, start=True, stop=True,
                       r=["mqk0", "mqk1"], w=[bk[6]])
                    op("dve", "scalar_tensor_tensor", SpT[:, :], banks[6][:, 0:128], e16[:, a:a + 1], tri[:, :],
                       ALU.mult, ALU.mult, r=[bk[6], "e16", "tri"], w=["SpT"])
                    op("pe", "transpose", bkb[7][:, 0:128], ks, identb[:, :], r=["mqk1", "identb"], w=[bk[7]])
                    op("act", "activation", K2[:, :], bkb[7][:, 0:128], AF.Copy, scale=e16[:, 4 + a:5 + a],
                       r=[bk[7], "e16"], w=["K2"])
                    op("pe", "matmul", banks[0][:, 0:129], qs, Cbf[:, :], start=True, stop=False,
                       r=["mqk0", "Cbf"], w=[bk[0]])
                    op("pe", "matmul", banks[0][:, 0:129], SpT[:, :], MV_g[:, a, :], start=False, stop=True,
                       r=["SpT", "MV_g"], w=[bk[0]])
                    op("pe", "matmul", banks[1][:, 0:129], K2[:, :], MV_g[:, a, :], start=True, stop=True,
                       r=["K2", "MV_g"], w=[bk[1]])
                    op("dve", "scalar_tensor_tensor", Cst[:, :], Cst[:, :], e16[:, 12 + a:13 + a], banks[1][:, 0:129],
                       ALU.mult, ALU.add, r=["Cst", "e16", bk[1]], w=["Cst"])
                    op("act", "activation", Cbf[:, :], Cst[:, :], AF.Copy, r=["Cst"], w=["Cbf"])
                    op("act", "activation", den[:, :], banks[0][:, 128:129], AF.Abs, r=[bk[0]], w=["den"])
                    op("dve", "tensor_tensor", den[:, :], den[:, :], e16[:, 8 + a:9 + a], ALU.max,
                       r=["den", "e16"], w=["den"])
                    op("dve", "reciprocal", rden[:, :], den[:, :], r=["den"], w=["rden"])
                    op("dve", "tensor_scalar", ex[:, a, 128:256], banks[0][:, 0:128], rden[:, 0:1], None, ALU.mult,
                       r=[bk[0], "rden"], w=[exk])
                if cut < 5:
                    continue
                nck = 4 * j + 4

                def emit_qk(ci):
                    kc = ci
                    pS = 2 + (ci % 3)
                    n = kc // 2
                    c0 = 0 if kc <= 4 * j else 128 * (kc - 4 * j)
                    op("pe", "matmul", banks[pS][:, c0:512], kT_all[:, 128 * kc:128 * kc + 128], qT_g[:, c0:512],
                       start=True, stop=False, r=["kT%d" % (kc // 4), "qT_g"], w=[bk[pS]])
                    if kc < 4 * j:
                        op("pe", "matmul", banks[pS][:, 0:512], esel[:, 128 * n:128 * n + 128], negmT[:, 0:512],
                           start=False, stop=True, r=["esel", "negmT"], w=[bk[pS]])
                    else:
                        d0 = 128 * (kc - 4 * j)
                        op("pe", "matmul", banks[pS][:, d0:d0 + 128], identb[:, :], trineg[:, :],
                           start=False, stop=(kc >= 4 * j + 2), r=["identb", "trineg"], w=[bk[pS]])
                        if kc < 4 * j + 2:
                            op("pe", "matmul", banks[pS][:, 256:512], esel[:, 128 * n:128 * n + 128],
                               negmT[:, 256:512], start=False, stop=True, r=["esel", "negmT"], w=[bk[pS]])
                    op("act", "activation", PT[ci % 3][:, c0:512], banks[pS][:, c0:512], AF.Exp, scale=ISQ,
                       r=[bk[pS]], w=["PT%d" % (ci % 3)])

                def emit_pv(ci):
                    kc = ci
                    c0 = 0 if kc <= 4 * j else 128 * (kc - 4 * j)
                    for a in range(c0 // 128, 4):
                        ob = 0 if a < 2 else 1
                        col = 129 * (a % 2)
                        op("pe", "matmul", banks[ob][:, col:col + 129], PT[ci % 3][:, 128 * a:128 * a + 128],
                           V_all[:, kc, :], start=(ci == 0 and a % 2 == 0), stop=(ci == nck - 1),
                           skip_group_check=True, r=["PT%d" % (ci % 3), "V%d" % (kc // 4), "V_all"], w=[bk[ob]])

                emit_qk(0)
                for ci in range(nck):
                    if ci + 1 < nck:
                        emit_qk(ci + 1)
                    emit_pv(ci)
                for a in range(4):
                    ob = 0 if a < 2 else 1
                    col = 129 * (a % 2)
                    op("dve", "reciprocal", rinv[:, a:a + 1], banks[ob][:, col + 128:col + 129], r=[bk[ob]], w=["rinv"])
                    op("dve", "tensor_scalar", ex[:, a, 0:128], banks[ob][:, col:col + 128], rinv[:, a:a + 1], None,
                       ALU.mult, r=[bk[ob], "rinv"], w=[exk])
                dma("sp", ex_b[512 * j:512 * j + 512, :].rearrange("(a p) c -> p a c", p=128), ex[:, :, :],
                    r=[exk], w=["ex_b"])
            cc(ex_b.ap(), ex_all.ap(), r=["ex_b"], w=["ex_all"])
            sch.flush()
        if stop == 1:
            return nc

        with ExitStack() as pb_:
            sbb = lambda n, sh, dt: pb_.enter_context(nc.sbuf_tensor(n, sh, dt))
            x1 = sbb("x1", [128, GT, D], F32)
            actT = sbb("actT", [128, 16, GB], BF16)
            hT = sbb("hT", [128, KF, GB], BF16)
            Mx = [sbb("Mx%d" % i, [128, 8, 384], BF16) for i in range(2)]
            hf = sbb("hf", [128, 8, 128], F32)
            hsq = sbb("hsq", [128, 8, 128], F32)
            mo = sbb("mo", [128, 1024], BF16)
            st4 = sbb("st4", [128, 16], F32)
            nwb = sbb("nwb", [128, 1024], F32)
            bc = [sbb("bc%d" % i, [128, D], F32) for i in range(2)]
            wo = [sbb("wo%d" % i, [128, 16, 256], BF16) for i in range(2)]
            wg = [sbb("wg%d" % i, [128, 16, 256], BF16) for i in range(2)]
            wd = [sbb("wd%d" % i, [128, 11, 512], BF16) for i in range(2)]
            sg = [sbb("sg%d" % i, [128, GB], F32) for i in range(2)]
            junk = sbb("junk", [128, D], BF16)
            ls = sbb("ls", [128, 8], F32)

            dma("sp", nwb[:, :], mlnw.ap().rearrange("a b -> (a b)").partition_broadcast(128), w=["nwb"])

            def layer_norm_inplace(i):
                xt = x1[:, i, :]
                op("dve", "reduce_sum", ls[:, 0:1], xt, AX.X, r=["x1"], w=["ls"])
                op("pool", "memset", ls[:, 1:2], 0.0, w=["ls"])
                op("act", "activation", junk[:, :], xt, AF.Square, accum_out=ls[:, 1:2], r=["x1", "ls"], w=["junk", "ls"])
                op("dve", "tensor_scalar", ls[:, 2:4], ls[:, 0:2], 1.0 / D, None, ALU.mult, r=["ls"], w=["ls"])
                op("dve", "tensor_tensor", ls[:, 4:5], ls[:, 2:3], ls[:, 2:3], ALU.mult, r=["ls"], w=["ls"])
                op("dve", "tensor_tensor", ls[:, 5:6], ls[:, 3:4], ls[:, 4:5], ALU.subtract, r=["ls"], w=["ls"])
                op("act", "activation", ls[:, 6:7], ls[:, 5:6], AF.Ln, bias=eps_t[:, 0:1], r=["ls", "eps_t"], w=["ls"])
                op("act", "activation", ls[:, 6:7], ls[:, 6:7], AF.Exp, scale=-0.5, r=["ls"], w=["ls"])
                op("dve", "scalar_tensor_tensor", ls[:, 7:8], ls[:, 2:3], -1.0, ls[:, 6:7], ALU.mult, ALU.mult,
                   r=["ls"], w=["ls"])
                op("dve", "tensor_scalar", xt, xt, ls[:, 6:7], ls[:, 7:8], ALU.mult, ALU.add,
                   r=["x1", "ls"], w=["x1"])
                op("dve", "tensor_tensor", xt, xt, bc[0][:, :], ALU.mult, r=["x1", "bc0"], w=["x1"])
                op("pool", "tensor_tensor", xt, xt, bc[1][:, :], ALU.add, r=["x1", "bc1"], w=["x1"])

            exv4 = ex_all.ap().rearrange("(r o t) c -> o t r c", r=8, o=NCORE)
            wguv = wgu_all.ap().rearrange("(k p) (f c) -> p k f c", p=128, c=256)
            for g in range(NGB):
                dma("pool", bc[0][:, :], ln1w.ap().rearrange("a b -> (a b)").partition_broadcast(128), w=["bc0"])
                dma("pool", bc[1][:, :], ln1b.ap().rearrange("a b -> (a b)").partition_broadcast(128), w=["bc1"])
                for i in range(GT):
                    ti = g * GT + i
                    mb = ti % 2
                    M = Mx[mb]
                    mk_ = "Mx%d" % mb
                    dma("sp", M[:, :, :], exv4[bass.ds(off, 1), 128 * ti:128 * ti + 128, :, :]
                        .rearrange("o t r c -> (o t) r c"), r=["ex_all"], w=[mk_])
                    dma("sp", x1[:, i, :], xs[128 * ti:128 * ti + 128, :], w=["x1"])
                    op("dve", "tensor_copy", hf[:, :, :], M[:, :, 128:256], r=[mk_], w=["hf"])
                    for h in range(4):
                        op("dve", "reduce_sum", st4[:, h:h + 1],
                           hf[:, 2 * h:2 * h + 2, :].rearrange("p r c -> p (r c)"), AX.X, r=["hf"], w=["st4"])
                    op("act", "activation", hsq[:, :, :], hf[:, :, :], AF.Square, r=["hf"], w=["hsq"])
                    for h in range(4):
                        op("dve", "reduce_sum", st4[:, 4 + h:5 + h],
                           hsq[:, 2 * h:2 * h + 2, :].rearrange("p r c -> p (r c)"), AX.X, r=["hsq"], w=["st4"])
                    op("dve", "tensor_scalar", st4[:, 0:8], st4[:, 0:8], 1.0 / 256, None, ALU.mult, r=["st4"], w=["st4"])
                    op("dve", "tensor_tensor", st4[:, 8:12], st4[:, 0:4], st4[:, 0:4], ALU.mult, r=["st4"], w=["st4"])
                    op("dve", "tensor_tensor", st4[:, 8:12], st4[:, 4:8], st4[:, 8:12], ALU.subtract,
                       r=["st4"], w=["st4"])
                    op("act", "activation", st4[:, 12:16], st4[:, 8:12], AF.Ln, bias=eps_t[:, 0:1],
                       r=["st4", "eps_t"], w=["st4"])
                    op("act", "activation", st4[:, 12:16], st4[:, 12:16], AF.Exp, scale=-0.5, r=["st4"], w=["st4"])
                    op("dve", "scalar_tensor_tensor", st4[:, 8:12], st4[:, 0:4], -1.0, st4[:, 12:16], ALU.mult,
                       ALU.mult, r=["st4"], w=["st4"])
                    for h in range(4):
                        op("dve", "tensor_scalar", hf[:, 2 * h:2 * h + 2, :], hf[:, 2 * h:2 * h + 2, :],
                           st4[:, 12 + h:13 + h], st4[:, 8 + h:9 + h], ALU.mult, ALU.add, r=["hf", "st4"], w=["hf"])
                    op("dve", "tensor_tensor", hsq[:, :, :], hf[:, :, :], M[:, :, 256:384], ALU.mult,
                       r=["hf", mk_], w=["hsq"])
                    op("pool", "tensor_tensor", mo[:, :], hsq[:, :, :].rearrange("p r c -> p (r c)"), nwb[:, :],
                       ALU.mult, r=["hsq", "nwb"], w=["mo"])
                    for q4 in range(4):
                        pb = q4 % 2
                        pT = bkb[pb]
                        for kk in range(4):
                            kf = 4 * q4 + kk
                            src = mo[:, 128 * kf:128 * kf + 128] if kf < 8 else M[:, kf - 8, 0:128]
                            op("pe", "transpose", pT[:, 128 * kk:128 * kk + 128], src, identb[:, :],
                               r=["mo", mk_, "identb"], w=[bk[pb]])
                        dst = actT[:, 4 * q4:4 * q4 + 4, 128 * i:128 * i + 128]
                        srcp = pT[:, 0:512].rearrange("p (k c) -> p k c", k=4)
                        if q4 % 2 == 0:
                            op("act", "activation", dst, srcp, AF.Copy, r=[bk[pb]], w=["actT"])
                        else:
                            op("dve", "tensor_copy", dst, srcp, r=[bk[pb]], w=["actT"])
                for ct in range(8):
                    wb = ct % 2
                    dma("sp", wo[wb][:, :, :],
                        wout_all[:, 256 * ct:256 * ct + 256].rearrange("(k p) c -> p k c", p=128),
                        r=["wout_all"], w=["wo%d" % wb])
                    for i in range(GT):
                        po = 2 + ((ct * GT + i) % 2)
                        for k in range(16):
                            op("pe", "matmul", banks[po][:, 0:256], actT[:, k, 128 * i:128 * i + 128], wo[wb][:, k, :],
                               start=(k == 0), stop=(k == 15), r=["actT", "wo%d" % wb], w=[bk[po]])
                        op("dve", "scalar_tensor_tensor", x1[:, i, 256 * ct:256 * ct + 256],
                           x1[:, i, 256 * ct:256 * ct + 256], DN_ALPHA, banks[po][:, 0:256], ALU.mult, ALU.add,
                           r=["x1", bk[po]], w=["x1"])
                for i in range(GT):
                    layer_norm_inplace(i)
                for k in range(16):
                    pb = k % 2
                    for i in range(GT):
                        op("pe", "transpose", banks[pb][:, 128 * i:128 * i + 128], x1[:, i, 128 * k:128 * k + 128],
                           identf[:, :], r=["x1", "identf"], w=[bk[pb]])
                    op("act", "activation", actT[:, k, :], banks[pb][:, 0:GB], AF.Identity, bias=adaT[:, 48 + k:49 + k],
                       scale=sc2p[:, k:k + 1], r=[bk[pb], "adaT", "sc2p"], w=["actT"])
                for f in range(KF):
                    wb = f % 2
                    dma("sp" if f % 2 == 0 else "pool", wg[wb][:, :, :], wguv[:, :, f, :], r=["wgu_all"], w=["wg%d" % wb])
                    pg = 4 + 2 * (f % 2)
                    pu = pg + 1
                    for k in range(16):
                        op("pe", "matmul", banks[pg][:, 0:GB], wg[wb][:, k, 0:128], actT[:, k, :],
                           start=(k == 0), stop=(k == 15), r=["wg%d" % wb, "actT"], w=[bk[pg]])
                    for k in range(16):
                        op("pe", "matmul", banks[pu][:, 0:GB], wg[wb][:, k, 128:256], actT[:, k, :],
                           start=(k == 0), stop=(k == 15), r=["wg%d" % wb, "actT"], w=[bk[pu]])
                    op("act", "activation", sg[wb][:, :], banks[pg][:, 0:GB], AF.Silu, r=[bk[pg]], w=["sg%d" % wb])
                    op("dve", "tensor_tensor", hT[:, f, :], sg[wb][:, :], banks[pu][:, 0:GB], ALU.mult,
                       r=["sg%d" % wb, bk[pu]], w=["hT"])
                pc = 0
                for ct in range(4):
                    for q4 in range(4):
                        wb = pc % 2
                        pc += 1
                        dma("sp", wd[wb][:, :, :],
                            wdn_all[128 * 11 * q4:128 * 11 * (q4 + 1), 512 * ct:512 * ct + 512]
                            .rearrange("(k p) c -> p k c", p=128), r=["wdn_all"], w=["wd%d" % wb])
                        for i in range(GT):
                            for kk in range(11):
                                op("pe", "matmul", banks[i][:, :], hT[:, 11 * q4 + kk, 128 * i:128 * i + 128],
                                   wd[wb][:, kk, :], start=(q4 == 0 and kk == 0), stop=(q4 == 3 and kk == 10),
                                   r=["hT", "wd%d" % wb], w=[bk[i]])
                    for i in range(GT):
                        op("dve", "scalar_tensor_tensor", x1[:, i, 512 * ct:512 * ct + 512],
                           x1[:, i, 512 * ct:512 * ct + 512], DN_ALPHA, banks[i][:, :], ALU.mult, ALU.add,
                           r=["x1", bk[i]], w=["x1"])
                dma("pool", bc[0][:, :], ln2w.ap().rearrange("a b -> (a b)").partition_broadcast(128), w=["bc0"])
                dma("pool", bc[1][:, :], ln2b.ap().rearrange("a b -> (a b)").partition_broadcast(128), w=["bc1"])
                for i in range(GT):
                    ti = g * GT + i
                    layer_norm_inplace(i)
                    dma("sp", y[128 * ti:128 * ti + 128, :], x1[:, i, :], r=["x1"], w=["y"])
            sch.flush()
    return nc


def _consts(S):
    bf = ml_dtypes.bfloat16
    idx = np.arange(128)
    c = {}
    c["c_identb"] = np.eye(128, dtype=np.float32).astype(bf)
    c["c_identf"] = np.eye(128, dtype=np.float32)
    c["c_tri"] = (idx[:, None] <= idx[None, :]).astype(np.float32)
    c["c_ones"] = np.ones((128, 128), np.float32)
    c["c_trineg"] = np.where(idx[:, None] > idx[None, :], NEG, 0.0).astype(np.float32).astype(bf)
    es = np.zeros((64, 64 * 128), np.float32)
    for n in range(64):
        es[n, 128 * n:128 * n + 128] = 1.0
    c["c_esel"] = es.astype(bf)
    rp = np.zeros((128, 32), np.float32)
    for m in range(16):
        rp[m + 16, m] = 1.0
        rp[m, m + 16] = 1.0
    c["c_ropep"] = rp
    half = 16
    inv = (np.float32(ROPE_THETA) ** (-np.arange(half, dtype=np.float32) * np.float32(2.0) / np.float32(32))).astype(np.float32)
    ang = (np.arange(S, dtype=np.float32)[:, None] * inv[None, :]).astype(np.float32)
    cs = np.cos(ang).astype(np.float32).T
    sn = np.sin(ang).astype(np.float32).T
    c["c_cos"] = np.ascontiguousarray(np.concatenate([cs, cs], 0))
    c["c_sin"] = np.ascontiguousarray(np.concatenate([-sn, sn], 0))
    return c


def make_in_maps(S, x, c, w_ada, b_ada, w_in, conv_w, conv_b, ml_igate_b, ml_fgate_b, ml_norm_w,
                 w_out, ln1_w, ln1_b, w_gu, w_down, ln2_w, ln2_b):
    TPC = S // NCORE
    f32 = np.float32
    x = np.asarray(x, f32)[0]
    w_ada = np.asarray(w_ada, f32)[0]
    b_ada = np.asarray(b_ada, f32)[0]
    w_in = np.asarray(w_in, f32)[0]
    conv_w = np.asarray(conv_w, f32)[0]
    conv_b = np.asarray(conv_b, f32)[0]
    igb = np.asarray(ml_igate_b, f32)[0]
    fgb = np.asarray(ml_fgate_b, f32)[0]
    w_out = np.asarray(w_out, f32)[0]
    w_gu = np.asarray(w_gu, f32)[0]
    w_down = np.asarray(w_down, f32)[0]
    consts = _consts(S)
    wgu_p = np.ascontiguousarray(
        w_gu.reshape(D, 2, KF, 128).transpose(0, 2, 1, 3).reshape(D, 2 * DFF))
    wdn_p = np.concatenate([w_down, np.zeros((KFP * 128 - DFF, D), f32)], 0)
    maps = []
    for cc in range(NCORE):
        h, half = cc // 2, cc % 2
        cols = np.concatenate([
            np.arange(128 * cc, 128 * cc + 128),
            1024 + np.arange(128 * cc, 128 * cc + 128),
            3072 + np.arange(128 * h, 128 * h + 128),
            3584 + np.arange(128 * h, 128 * h + 128),
            2048 + np.arange(128 * cc, 128 * cc + 128),
            4096 + 256 * h + 128 * half + np.arange(128),
            5120 + 256 * h + 128 * half + np.arange(128),
            np.array([6144 + h, 6148 + h]),
        ])
        cwq = conv_w[:, 128 * h:128 * h + 128].T
        cwk = conv_w[:, 512 + 128 * h:512 + 128 * h + 128].T
        m = {
            "xs": np.ascontiguousarray(x[cc * TPC:(cc + 1) * TPC]),
            "cvec": np.ascontiguousarray(np.asarray(c, f32).reshape(16, 128)),
            "wada": np.ascontiguousarray(w_ada[:, 1536 * cc:1536 * cc + 1536]),
            "bada": np.ascontiguousarray(b_ada[1536 * cc:1536 * cc + 1536].reshape(1, 1536)),
            "win": np.ascontiguousarray(w_in[:, cols]),
            "convw": np.ascontiguousarray(np.concatenate([cwq, cwk], 1)),
            "convb": np.ascontiguousarray(np.stack([conv_b[128 * h:128 * h + 128],
                                                    conv_b[512 + 128 * h:512 + 128 * h + 128]], 1)),
            "gateb": np.ascontiguousarray(np.broadcast_to(np.array([igb[h], fgb[h]], f32)[None, :], (128, 2))),
            "wout_s": np.ascontiguousarray(w_out[256 * cc:256 * cc + 256]),
            "wgu_s": np.ascontiguousarray(wgu_p[256 * cc:256 * cc + 256]),
            "wdn_s": np.ascontiguousarray(wdn_p[768 * cc:768 * cc + 768]),
            "ln1w": np.asarray(ln1_w, f32).reshape(1, D), "ln1b": np.asarray(ln1_b, f32).reshape(1, D),
            "ln2w": np.asarray(ln2_w, f32).reshape(1, D), "ln2b": np.asarray(ln2_b, f32).reshape(1, D),
            "mlnw": np.asarray(ml_norm_w, f32).reshape(1, 1024),
            "cidx": np.array([[cc]], np.int32),
        }
        m.update(consts)
        maps.append(m)
    return maps


_NC_CACHE = {}


def run(S, **inputs):
    if S not in _NC_CACHE:
        _NC_CACHE[S] = build(S)
    nc = _NC_CACHE[S]
    maps = make_in_maps(S, **inputs)
    res = run_bass_kernel_spmd(nc, maps, core_ids=list(range(NCORE)))
    out = np.concatenate([np.asarray(r["y"], np.float32) for r in res.results], 0)
    return out[None]


def kernel(**inputs):
    S = int(np.asarray(inputs["x"]).shape[1])
    return run(S, **inputs)
```

```python
import math
from contextlib import ExitStack

import numpy as np
import ml_dtypes

import concourse.bass as bass
import concourse.mybir as mybir
from concourse.bass_utils import run_bass_kernel_spmd

F32 = mybir.dt.float32
BF16 = mybir.dt.bfloat16
I32 = mybir.dt.int32
AF = mybir.ActivationFunctionType
ALU = mybir.AluOpType
AX = mybir.AxisListType

NCORE = 8
D = 2048
KD = 16
DFF = 5632
KF = 44
KFP = 48
NCOL = 898
LN_EPS = 1e-5
DEPTH = 1
DN_ALPHA = (2 * DEPTH) ** 0.25
NEG = -30000.0
ROPE_THETA = 500000.0


class Sched:
    NSLOT = 6
    ENG = {"pe": "tensor", "act": "scalar", "dve": "vector", "pool": "gpsimd", "sp": "sync"}

    def __init__(self, nc, es):
        self.nc = nc
        self.ops = []
        self.esem = {e: es.enter_context(nc.semaphore("s_" + e)) for e in self.ENG}
        self.dq = {e: [es.enter_context(nc.semaphore("d_%s%d" % (e, s))) for s in range(self.NSLOT)]
                   for e in ("sp", "pool", "act")}
        self.ccs = [es.enter_context(nc.semaphore("cc%d" % s)) for s in range(8)]
        self.cci = 0
        self.ecount = {e: 0 for e in self.ENG}
        self.dcount = {e: [0] * self.NSLOT for e in self.dq}
        self.dlast = {e: [None] * self.NSLOT for e in self.dq}
        self.drr = {e: 0 for e in self.dq}
        self.waited = {e: {} for e in self.ENG}

    def op(self, eng, meth, *args, r=(), w=(), **kw):
        bw = tuple(k for k in r if len(k) == 2 and k[0] == "b" and k[1].isdigit() and k not in w)
        self.ops.append(dict(eng=eng, meth=meth, args=args, kw=kw, r=tuple(r), w=tuple(w) + bw, kind="c"))

    def dma(self, eng, out, in_, r=(), w=()):
        self.ops.append(dict(eng=eng, meth="dma_start", args=(), kw=dict(out=out, in_=in_),
                             r=tuple(r), w=tuple(w), kind="d"))

    def cc(self, src, dst, r=(), w=()):
        self.ops.append(dict(eng="pool", meth="collective_compute",
                             args=("AllGather", ALU.bypass),
                             kw=dict(replica_groups=[list(range(NCORE))], ins=[src], outs=[dst]),
                             r=tuple(r), w=tuple(w), kind="cc"))

    def _wait(self, ename, ev):
        sem, val, _ = ev
        key = id(sem)
        if self.waited[ename].get(key, 0) >= val:
            return
        getattr(self.nc, self.ENG[ename]).wait_ge(sem, val)
        self.waited[ename][key] = val

    def flush(self):
        nc = self.nc
        ops = self.ops
        self.ops = []
        last_w = {}
        readers = {}
        for i, o in enumerate(ops):
            deps = set()
            for k in o["r"]:
                if k in last_w:
                    deps.add(last_w[k])
            for k in o["w"]:
                if k in last_w:
                    deps.add(last_w[k])
                deps.update(readers.get(k, ()))
            deps.discard(i)
            for k in o["r"]:
                readers.setdefault(k, []).append(i)
            for k in o["w"]:
                last_w[k] = i
                readers[k] = []
            o["deps"] = {d for d in deps
                         if not (ops[d]["eng"] == "pe" and o["eng"] == "pe"
                                 and ops[d]["kind"] == "c" and o["kind"] == "c")}
        signaled = set()
        for o in ops:
            signaled.update(o["deps"])
        lastc = {}
        for i, o in enumerate(ops):
            if o["kind"] == "c":
                lastc[o["eng"]] = i
        signaled.update(lastc.values())
        finals = []
        for i, o in enumerate(ops):
            e = o["eng"]
            if o["kind"] == "c":
                if i in signaled:
                    self.ecount[e] += 1
                    o["ev"] = (self.esem[e], self.ecount[e], 1)
                else:
                    o["ev"] = None
                o["prev"] = None
            elif o["kind"] == "d":
                s = self.drr[e]
                self.drr[e] = (s + 1) % self.NSLOT
                o["prev"] = self.dlast[e][s]
                self.dcount[e][s] += 16
                o["ev"] = (self.dq[e][s], self.dcount[e][s], 16)
                self.dlast[e][s] = o["ev"]
            else:
                o["ev"] = (self.ccs[self.cci], 1, 1)
                self.cci += 1
                o["prev"] = None
                finals.append(o["ev"])
        for i, o in enumerate(ops):
            e = o["eng"]
            for d in sorted(o["deps"]):
                self._wait(e, ops[d]["ev"])
            if o["prev"] is not None:
                self._wait(e, o["prev"])
            ins = getattr(getattr(nc, self.ENG[e]), o["meth"])(*o["args"], **o["kw"])
            if o["ev"] is not None:
                ins.then_inc(o["ev"][0], o["ev"][2])
        for e, i in lastc.items():
            self._wait(e, ops[i]["ev"])
        for e in self.dq:
            for s in range(self.NSLOT):
                if self.dlast[e][s] is not None:
                    self._wait("sp", self.dlast[e][s])
        for ev in finals:
            self._wait("sp", ev)
        for e, i in lastc.items():
            self._wait("sp", ops[i]["ev"])
        nc.all_engine_barrier()


def build(S, stop=9, cut=9):
    TPC = S // NCORE
    NT128 = S // 128
    NG = S // 512
    GB = min(512, TPC)
    GT = GB // 128
    NGB = TPC // GB
    assert S % 512 == 0 and TPC % 128 == 0 and TPC % GB == 0

    nc = bass.Bass("TRN2", target_bir_lowering=False)
    di = lambda n, sh, dt: nc.dram_tensor(n, sh, dt, kind="ExternalInput")
    xs = di("xs", [TPC, D], F32)
    cvec = di("cvec", [16, 128], F32)
    wada = di("wada", [D, 1536], F32)
    bada = di("bada", [1, 1536], F32)
    win = di("win", [D, NCOL], F32)
    convw = di("convw", [128, 8], F32)
    convb = di("convb", [128, 2], F32)
    gateb = di("gateb", [128, 2], F32)
    wout_s = di("wout_s", [256, D], F32)
    wgu_s = di("wgu_s", [256, 2 * DFF], F32)
    wdn_s = di("wdn_s", [768, D], F32)
    ln1w = di("ln1w", [1, D], F32)
    ln1b = di("ln1b", [1, D], F32)
    ln2w = di("ln2w", [1, D], F32)
    ln2b = di("ln2b", [1, D], F32)
    mlnw = di("mlnw", [1, 1024], F32)
    c_identb = di("c_identb", [128, 128], BF16)
    c_identf = di("c_identf", [128, 128], F32)
    c_tri = di("c_tri", [128, 128], F32)
    c_ones = di("c_ones", [128, 128], F32)
    c_trineg = di("c_trineg", [128, 128], BF16)
    c_esel = di("c_esel", [64, 64 * 128], BF16)
    c_ropep = di("c_ropep", [128, 32], F32)
    c_cos = di("c_cos", [32, S], F32)
    c_sin = di("c_sin", [32, S], F32)
    cidx = di("cidx", [1, 1], I32)
    y = nc.dram_tensor("y", [TPC, D], F32, kind="ExternalOutput")

    ada_b = nc.dram_tensor("ada_b", [1, 1536], F32)
    ada_all = nc.dram_tensor("ada_all", [8, 1536], F32)
    x_b = nc.dram_tensor("x_b", [TPC, D], BF16)
    x_all = nc.dram_tensor("x_all", [S, D], BF16)
    wout_b = nc.dram_tensor("wout_b", [256, D], BF16)
    wout_all = nc.dram_tensor("wout_all", [D, D], BF16)
    wgu_b = nc.dram_tensor("wgu_b", [256, 2 * DFF], BF16)
    wgu_all = nc.dram_tensor("wgu_all", [D, 2 * DFF], BF16)
    wdn_b = nc.dram_tensor("wdn_b", [768, D], BF16)
    wdn_all = nc.dram_tensor("wdn_all", [KFP * 128, D], BF16)
    ex_b = nc.dram_tensor("ex_b", [S, 384], BF16)
    ex_all = nc.dram_tensor("ex_all", [8 * S, 384], BF16)

    ISQ = 1.0 / math.sqrt(128.0)
    LNC = math.log(ISQ)

    with ExitStack() as top:
        sch = Sched(nc, top)
        op, dma, cc = sch.op, sch.dma, sch.cc
        rcid = top.enter_context(nc.sync.register("rcid"))
        nc.sync.reg_load(rcid, cidx[0:1, 0:1])
        off = nc.sync.snap(rcid, min_val=0, max_val=NCORE - 1)

        sb = lambda n, sh, dt: top.enter_context(nc.sbuf_tensor(n, sh, dt))
        banks = [top.enter_context(nc.psum_tensor("bank%d" % i, [128, 512], F32)) for i in range(8)]
        bkb = [bk_[:, :].bitcast(BF16) for bk_ in banks]
        bk = ["b%d" % i for i in range(8)]
        identb = sb("identb", [128, 128], BF16)
        identf = sb("identf", [128, 128], F32)
        tri = sb("tri", [128, 128], F32)
        onesf = sb("onesf", [128, 128], F32)
        adaT = sb("adaT", [128, 96], F32)
        sc1p = sb("sc1p", [128, 16], F32)
        sc2p = sb("sc2p", [128, 16], F32)
        lnc_t = sb("lnc_t", [128, 1], F32)
        one_t = sb("one_t", [128, 1], F32)

        for t_, s_, n_ in ((identb, c_identb, "identb"), (identf, c_identf, "identf"), (tri, c_tri, "tri"),
                           (onesf, c_ones, "onesf")):
            dma("sp", t_[:, :], s_[:, :], w=[n_])
        op("pool", "memset", lnc_t[:, :], LNC, w=["lnc_t"])
        op("pool", "memset", one_t[:, :], 1.0, w=["one_t"])
        eps_t = sb("eps_t", [128, 1], F32)
        op("pool", "memset", eps_t[:, :], LN_EPS, w=["eps_t"])

        with ExitStack() as p0:
            sb0 = lambda n, sh, dt: p0.enter_context(nc.sbuf_tensor(n, sh, dt))
            wa = sb0("wa", [128, 16, 1536], F32)
            c16 = sb0("c16", [16, 128], F32)
            cT = sb0("cT", [128, 16], F32)
            scT = sb0("scT", [128, 16], F32)
            badat = sb0("badat", [1, 1536], F32)
            adas = sb0("adas", [1, 1536], F32)
            ada96 = sb0("ada96", [96, 128], F32)
            g1b = sb0("g1b", [128, D], F32)
            g2b = sb0("g2b", [128, D], F32)
            wo32 = sb0("wo32", [128, 2, D], F32)
            wo16 = sb0("wo16", [128, 2, D], BF16)
            wd32 = sb0("wd32", [128, 2, D], F32)
            wd16 = sb0("wd16", [128, 6, D], BF16)

            dma("pool", x_b[:, :], xs[:, :], w=["x_b"])
            dma("sp", wa[:, :, :], wada.ap().rearrange("(k p) c -> p k c", p=128), w=["wa"])
            dma("sp", c16[:, :], cvec[:, :], w=["c16"])
            dma("sp", badat[:, :], bada[:, :], w=["badat"])
            op("pe", "transpose", banks[0][:, 0:16], c16[:, :], identf[0:16, 0:16], r=["c16", "identf"], w=[bk[0]])
            op("dve", "tensor_copy", cT[:, :], banks[0][:, 0:16], r=[bk[0]], w=["cT"])
            op("act", "activation", scT[:, :], cT[:, :], AF.Silu, r=["cT"], w=["scT"])
            for ct in range(3):
                for k in range(16):
                    op("pe", "matmul", banks[1 + ct][0:1, :], scT[:, k:k + 1], wa[:, k, 512 * ct:512 * ct + 512],
                       start=(k == 0), stop=(k == 15), r=["scT", "wa"], w=[bk[1 + ct]])
                op("dve", "tensor_tensor", adas[0:1, 512 * ct:512 * ct + 512], banks[1 + ct][0:1, :],
                   badat[0:1, 512 * ct:512 * ct + 512], ALU.add, r=[bk[1 + ct], "badat"], w=["adas"])
            dma("sp", ada_b[:, :], adas[:, :], r=["adas"], w=["ada_b"])
            cc(ada_b.ap(), ada_all.ap(), r=["ada_b"], w=["ada_all"])
            cc(x_b.ap(), x_all.ap(), r=["x_b"], w=["x_all"])
            dma("pool", wgu_b[:, :], wgu_s[:, :], w=["wgu_b"])
            cc(wgu_b.ap(), wgu_all.ap(), r=["wgu_b"], w=["wgu_all"])
            adaflat = ada_all.ap().rearrange("a b -> (a b)")
            dma("sp", ada96[:, :], adaflat.rearrange("(m p) -> m p", p=128), r=["ada_all"], w=["ada96"])
            op("pe", "transpose", banks[0][:, 0:96], ada96[:, :], identf[0:96, 0:96], r=["ada96", "identf"], w=[bk[0]])
            op("dve", "tensor_copy", adaT[:, :], banks[0][:, 0:96], r=[bk[0]], w=["adaT"])
            op("dve", "tensor_scalar_add", sc1p[:, :], adaT[:, 16:32], 1.0, r=["adaT"], w=["sc1p"])
            op("dve", "tensor_scalar_add", sc2p[:, :], adaT[:, 64:80], 1.0, r=["adaT"], w=["sc2p"])
            dma("sp", g1b[:, :], adaflat[4096:6144].partition_broadcast(128), r=["ada_all"], w=["g1b"])
            dma("sp", g2b[:, :], adaflat[10240:12288].partition_broadcast(128), r=["ada_all"], w=["g2b"])
            op("pool", "tensor_scalar_add", g1b[:, :], g1b[:, :], 1.0, r=["g1b"], w=["g1b"])
            op("pool", "tensor_scalar_add", g2b[:, :], g2b[:, :], 1.0, r=["g2b"], w=["g2b"])
            dma("sp", wo32[:, :, :], wout_s.ap().rearrange("(a p) c -> p a c", p=128), w=["wo32"])
            for a in range(2):
                op("dve", "tensor_tensor", wo16[:, a, :], wo32[:, a, :], g1b[:, :], ALU.mult,
                   r=["wo32", "g1b"], w=["wo16"])
            dma("sp", wout_b.ap().rearrange("(a p) c -> p a c", p=128), wo16[:, :, :], r=["wo16"], w=["wout_b"])
            cc(wout_b.ap(), wout_all.ap(), r=["wout_b"], w=["wout_all"])
            wdv = wdn_s.ap().rearrange("(a p) c -> p a c", p=128)
            for h3 in range(3):
                dma("sp", wd32[:, :, :], wdv[:, 2 * h3:2 * h3 + 2, :], w=["wd32"])
                for a in range(2):
                    op("dve", "tensor_tensor", wd16[:, 2 * h3 + a, :], wd32[:, a, :], g2b[:, :], ALU.mult,
                       r=["wd32", "g2b"], w=["wd16"])
            dma("sp", wdn_b.ap().rearrange("(a p) c -> p a c", p=128), wd16[:, :, :], r=["wd16"], w=["wdn_b"])
            cc(wdn_b.ap(), wdn_all.ap(), r=["wdn_b"], w=["wdn_all"])
            sch.flush()
        if stop == 0:
            return nc

        with ExitStack() as pa:
            sba = lambda n, sh, dt: pa.enter_context(nc.sbuf_tensor(n, sh, dt))
            winb = sba("winb", [128, 16, NCOL], BF16)
            kT_all = sba("kT_all", [128, S], BF16)
            V_all = sba("V_all", [128, NT128, 129], BF16)
            esel = sba("esel", [64, 64 * 128], BF16)
            trineg = sba("trineg", [128, 128], BF16)
            ropep = sba("ropep", [128, 32], F32)
            cw = sba("cw", [128, 8], F32)
            cbias = sba("cbias", [128, 2], F32)
            gb_t = sba("gb_t", [128, 2], F32)
            xin = [sba("xin%d" % i, [128, 4, D], BF16) for i in range(2)]
            uT = sba("uT", [128, 16, 512], BF16)
            cos_t = [sba("cos_t%d" % i, [32, 512], F32) for i in range(2)]
            sin_t = [sba("sin_t%d" % i, [32, 512], F32) for i in range(2)]
            tq = sba("tq", [128, 512], F32)
            tk = sba("tk", [128, 512], F32)
            r1 = sba("r1", [32, 512], F32)
            r2 = sba("r2", [32, 512], F32)
            qT_g = sba("qT_g", [128, 512], BF16)
            ksum = sba("ksum", [128, 64], F32)
            gt = sba("gt", [128, 64], F32)
            m8 = sba("m8", [128, 8], F32)
            negm = sba("negm", [128, 64], BF16)
            negmT = sba("negmT", [64, 512], BF16)
            cb = [sba("cb%d" % i, [128, 515], F32) for i in range(2)]
            acc = sba("acc", [128, 512], F32)
            mqk = [sba("mqk%d" % i, [128, 512], BF16) for i in range(2)]
            MV_g = sba("MV_g", [128, 4, 129], BF16)
            exs = [sba("exs%d" % i, [128, 4, 384], BF16) for i in range(2)]
            gts = sba("gts", [128, 4, 2], F32)
            lt = sba("lt", [128, 4], F32)
            et = sba("et", [128, 4], F32)
            cum = sba("cum", [128, 8], F32)
            a16 = sba("a16", [128, 16], F32)
            e16 = sba("e16", [128, 16], F32)
            SpT = sba("SpT", [128, 128], BF16)
            K2 = sba("K2", [128, 128], BF16)
            Cst = sba("Cst", [128, 129], F32)
            Cbf = sba("Cbf", [128, 129], BF16)
            den = sba("den", [128, 1], F32)
            rden = sba("rden", [128, 1], F32)
            PT = [sba("PT%d" % i, [128, 512], BF16) for i in range(3)]
            rinv = sba("rinv", [128, 4], F32)

            op("pool", "memset", Cst[:, :], 0.0, w=["Cst"])
            op("pool", "memset", Cbf[:, :], 0.0, w=["Cbf"])
            op("pool", "memset", ksum[:, :], 0.0, w=["ksum"])
            op("pool", "memset", V_all[:, :, 128:129], 1.0, w=["V_all"])
            op("pool", "memset", MV_g[:, :, 128:129], 1.0, w=["MV_g"])
            op("pool", "memset", cb[0][:, 0:3], 0.0, w=["cb0"])
            op("pool", "memset", cb[1][:, 0:3], 0.0, w=["cb1"])
            dma("pool", winb[:, :, :], win.ap().rearrange("(k p) c -> p k c", p=128), w=["winb"])
            dma("sp", esel[:, :], c_esel[:, :], w=["esel"])
            dma("sp", trineg[:, :], c_trineg[:, :], w=["trineg"])
            dma("sp", ropep[:, :], c_ropep[:, :], w=["ropep"])
            dma("sp", cw[:, :], convw[:, :], w=["cw"])
            dma("sp", cbias[:, :], convb[:, :], w=["cbias"])
            dma("sp", gb_t[:, :], gateb[:, :], w=["gb_t"])

            def load_x(j):
                b = j % 2
                dma("sp", xin[b][:, :, :], x_all[512 * j:512 * j + 512, :].rearrange("(a p) d -> p a d", p=128),
                    r=["x_all"], w=["xin%d" % b])
                dma("sp", cos_t[b][:, :], c_cos[:, 512 * j:512 * j + 512], w=["cos%d" % b])
                dma("sp", sin_t[b][:, :], c_sin[:, 512 * j:512 * j + 512], w=["sin%d" % b])

            def rope(tt, tname, b, psb):
                op("pe", "matmul", banks[psb][0:32, :], ropep[:, :], tt[:, :], start=True, stop=True,
                   r=[tname, "ropep"], w=[bk[psb]])
                op("dve", "tensor_tensor", r1[:, :], tt[0:32, :], cos_t[b][:, :], ALU.mult,
                   r=[tname, "cos%d" % b], w=["r1"])
                op("dve", "tensor_tensor", r2[:, :], banks[psb][0:32, :], sin_t[b][:, :], ALU.mult,
                   r=[bk[psb], "sin%d" % b], w=["r2"])
                op("dve", "tensor_tensor", tt[0:32, :], r1[:, :], r2[:, :], ALU.add, r=["r1", "r2"], w=[tname])

            load_x(0)
            for j in range(NG):
                b = j % 2
                if j + 1 < NG:
                    load_x(j + 1)
                ex = exs[b]
                exk = "exs%d" % b
                xk = "xin%d" % b
                for k in range(16):
                    pb = k % 2
                    pT = bkb[pb]
                    for a in range(4):
                        op("pe", "transpose", pT[:, 128 * a:128 * a + 128], xin[b][:, a, 128 * k:128 * k + 128],
                           identb[:, :], r=[xk, "identb"], w=[bk[pb]])
                    if k % 2 == 0:
                        op("act", "activation", uT[:, k, :], pT[:, 0:512], AF.Identity, bias=adaT[:, k:k + 1],
                           scale=sc1p[:, k:k + 1], r=[bk[pb], "adaT", "sc1p"], w=["uT"])
                    else:
                        op("dve", "tensor_scalar", uT[:, k, :], pT[:, 0:512], sc1p[:, k:k + 1], adaT[:, k:k + 1],
                           ALU.mult, ALU.add, r=[bk[pb], "adaT", "sc1p"], w=["uT"])
                if cut < 0.4:
                    continue
                for gi in (1, 0, 2, 3):
                    pf = 2 + (gi % 2)
                    if cut == 0.5 and gi != 1:
                        continue
                    if cut == 0.6 and gi not in (2, 3):
                        continue
                    if cut == 0.7 and gi not in (1, 0):
                        continue
                    for k in range(16):
                        op("pe", "matmul", banks[pf][:, :], winb[:, k, 128 * gi:128 * gi + 128], uT[:, k, :],
                           start=(k == 0), stop=(k == 15), r=["winb", "uT"], w=[bk[pf]])
                    if gi == 1:
                        op("act", "activation", tk[:, :], banks[pf][:, :], AF.Copy, r=[bk[pf]], w=["tk"])
                        rope(tk, "tk", b, 6)
                        for hh in range(2):
                            op("dve", "reduce_sum", ksum[:, 2 * j + hh:2 * j + hh + 1],
                               tk[:, 256 * hh:256 * hh + 256], AX.X, r=["tk"], w=["ksum"])
                        op("act", "activation", kT_all[:, 512 * j:512 * j + 512], tk[:, :], AF.Copy,
                           r=["tk"], w=["kT%d" % j])
                    elif gi == 0:
                        op("act", "activation", tq[:, :], banks[pf][:, :], AF.Copy, r=[bk[pf]], w=["tq"])
                        rope(tq, "tq", b, 6)
                        op("act", "activation", qT_g[:, :], tq[:, :], AF.Copy, r=["tq"], w=["qT_g"])
                        for a in range(4):
                            blk = 2 * j + a // 2
                            op("pe", "matmul", banks[6][:, 0:64], tq[:, 128 * a:128 * a + 128], ksum[:, :],
                               start=True, stop=True, r=["tq", "ksum"], w=[bk[6]])
                            op("dve", "tensor_copy", gt[:, :], banks[6][:, 0:64], r=[bk[6]], w=["gt"])
                            if blk < 64:
                                op("dve", "memset", gt[:, blk:64], -1e30, w=["gt"])
                            if blk >= 3:
                                op("dve", "max", m8[:, :], gt[:, :], r=["gt"], w=["m8"])
                                op("dve", "tensor_scalar", negm[:, :], gt[:, :], m8[:, 2:3], NEG, ALU.is_lt, ALU.mult,
                                   r=["gt", "m8"], w=["negm"])
                            else:
                                op("dve", "tensor_scalar", negm[:, :], gt[:, :], -1e29, NEG, ALU.is_lt, ALU.mult,
                                   r=["gt"], w=["negm"])
                            op("pe", "transpose", bkb[7][0:64, 0:128], negm[:, :], identb[:, :],
                               r=["negm", "identb"], w=[bk[7]])
                            op("dve", "tensor_copy", negmT[:, 128 * a:128 * a + 128], bkb[7][0:64, 0:128],
                               r=[bk[7]], w=["negmT"])
                    else:
                        w_ = gi - 2
                        ck = "cb%d" % w_
                        op("act", "activation", cb[w_][:, 3:515], banks[pf][:, :], AF.Copy, r=[bk[pf]], w=[ck])
                        op("dve", "tensor_scalar", acc[:, :], cb[w_][:, 0:512], cw[:, 4 * w_:4 * w_ + 1], None,
                           ALU.mult, r=[ck, "cw"], w=["acc"])
                        for t in range(1, 4):
                            op("dve", "scalar_tensor_tensor", acc[:, :], cb[w_][:, t:t + 512],
                               cw[:, 4 * w_ + t:4 * w_ + t + 1], acc[:, :], ALU.mult, ALU.add,
                               r=[ck, "cw", "acc"], w=["acc"])
                        op("act", "activation", mqk[w_][:, :], acc[:, :], AF.Silu, bias=cbias[:, w_:w_ + 1],
                           r=["acc", "cbias"], w=["mqk%d" % w_])
                        op("pool", "tensor_copy", cb[w_][:, 0:3], cb[w_][:, 512:515], r=[ck], w=[ck])
                if cut < 2:
                    continue
                for a in range(4):
                    pm = 4 + (a % 2)
                    for k in range(16):
                        op("pe", "matmul", banks[pm][:, 0:386], uT[:, k, 128 * a:128 * a + 128], winb[:, k, 512:898],
                           start=(k == 0), stop=(k == 15), r=["uT", "winb"], w=[bk[pm]])
                    op("act", "activation", V_all[:, 4 * j + a, 0:128], banks[pm][:, 0:128], AF.Copy,
                       r=[bk[pm]], w=["V%d" % j])
                    op("dve", "tensor_copy", MV_g[:, a, 0:128], banks[pm][:, 128:256], r=[bk[pm]], w=["MV_g"])
                    op("act", "activation", ex[:, a, 256:384], banks[pm][:, 256:384], AF.Sigmoid, r=[bk[pm]], w=[exk])
                    op("dve", "tensor_tensor", gts[:, a, :], banks[pm][:, 384:386], gb_t[:, :], ALU.add,
                       r=[bk[pm], "gb_t"], w=["gts"])
                if cut < 3:
                    continue
                op("act", "activation", et[:, :], gts[:, :, 1], AF.Exp, scale=-1.0, r=["gts"], w=["et"])
                op("act", "activation", lt[:, :], et[:, :], AF.Ln, bias=one_t[:, 0:1], r=["et", "one_t"], w=["lt"])
                op("pe", "matmul", banks[6][:, 0:4], tri[:, :], lt[:, :], start=True, stop=True,
                   r=["tri", "lt"], w=[bk[6]])
                op("pe", "matmul", banks[6][:, 4:8], onesf[:, :], lt[:, :], start=False, stop=True,
                   skip_group_check=True, r=["onesf", "lt"], w=[bk[6]])
                op("dve", "tensor_copy", cum[:, :], banks[6][:, 0:8], r=[bk[6]], w=["cum"])
                op("dve", "tensor_tensor", a16[:, 0:4], gts[:, :, 0], cum[:, 0:4], ALU.add, r=["gts", "cum"], w=["a16"])
                op("dve", "tensor_tensor", a16[:, 4:8], a16[:, 0:4], cum[:, 4:8], ALU.subtract,
                   r=["a16", "cum"], w=["a16"])
                op("dve", "tensor_copy", a16[:, 8:12], cum[:, 0:4], r=["cum"], w=["a16"])
                op("dve", "tensor_scalar", a16[:, 12:16], cum[:, 4:8], -1.0, None, ALU.mult, r=["cum"], w=["a16"])
                op("act", "activation", e16[:, 0:8], a16[:, 0:8], AF.Exp, bias=lnc_t[:, 0:1],
                   r=["a16", "lnc_t"], w=["e16"])
                op("act", "activation", e16[:, 8:16], a16[:, 8:16], AF.Exp, r=["a16"], w=["e16"])
                if cut < 4:
                    continue
                for a in range(4):
                    qs = mqk[0][:, 128 * a:128 * a + 128]
                    ks = mqk[1][:, 128 * a:128 * a + 128]
                    op("pe", "matmul", banks[6][:, 0:128], ks, qs, start=True, stop=True,
                       r=["mqk0", "mqk1"], w=[bk[6]])
                    op("dve", "scalar_tensor_tensor", SpT[:, :], banks[6][:, 0:128], e16[:, a:a + 1], tri[:, :],
                       ALU.mult, ALU.mult, r=[bk[6], "e16", "tri"], w=["SpT"])
                    op("pe", "transpose", bkb[7][:, 0:128], ks, identb[:, :], r=["mqk1", "identb"], w=[bk[7]])
                    op("act", "activation", K2[:, :], bkb[7][:, 0:128], AF.Copy, scale=e16[:, 4 + a:5 + a],
                       r=[bk[7], "e16"], w=["K2"])
                    op("pe", "matmul", banks[0][:, 0:129], qs, Cbf[:, :], start=True, stop=False,
                       r=["mqk0", "Cbf"], w=[bk[0]])
                    op("pe", "matmul", banks[0][:, 0:129], SpT[:, :], MV_g[:, a, :], start=False, stop=True,
                       r=["SpT", "MV_g"], w=[bk[0]])
                    op("pe", "matmul", banks[1][:, 0:129], K2[:, :], MV_g[:, a, :], start=True, stop=True,
                       r=["K2", "MV_g"], w=[bk[1]])
                    op("dve", "scalar_tensor_tensor", Cst[:, :], Cst[:, :], e16[:, 12 + a:13 + a], banks[1][:, 0:129],
                       ALU.mult, ALU.add, r=["Cst", "e16", bk[1]], w=["Cst"])
                    op("act", "activation", Cbf[:, :], Cst[:, :], AF.Copy, r=["Cst"], w=["Cbf"])
                    op("act", "activation", den[:, :], banks[0][:, 128:129], AF.Abs, r=[bk[0]], w=["den"])
                    op("dve", "tensor_tensor", den[:, :], den[:, :], e16[:, 8 + a:9 + a], ALU.max,
                       r=["den", "e16"], w=["den"])
                    op("dve", "reciprocal", rden[:, :], den[:, :], r=["den"], w=["rden"])
                    op("dve", "tensor_scalar", ex[:, a, 128:256], banks[0][:, 0:128], rden[:, 0:1], None, ALU.mult,
                       r=[bk[0], "rden"], w=[exk])
                if cut < 5:
                    continue
                nck = 4 * j + 4

                def emit_qk(ci):
                    kc = ci
                    pS = 2 + (ci % 3)
                    n = kc // 2
                    c0 = 0 if kc <= 4 * j else 128 * (kc - 4 * j)
                    op("pe", "matmul", banks[pS][:, c0:512], kT_all[:, 128 * kc:128 * kc + 128], qT_g[:, c0:512],
                       start=True, stop=False, r=["kT%d" % (kc // 4), "qT_g"], w=[bk[pS]])
                    if kc < 4 * j:
                        op("pe", "matmul", banks[pS][:, 0:512], esel[:, 128 * n:128 * n + 128], negmT[:, 0:512],
                           start=False, stop=True, r=["esel", "negmT"], w=[bk[pS]])
                    else:
                        d0 = 128 * (kc - 4 * j)
                        op("pe", "matmul", banks[pS][:, d0:d0 + 128], identb[:, :], trineg[:, :],
                           start=False, stop=(kc >= 4 * j + 2), r=["identb", "trineg"], w=[bk[pS]])
                        if kc < 4 * j + 2:
                            op("pe", "matmul", banks[pS][:, 256:512], esel[:, 128 * n:128 * n + 128],
                               negmT[:, 256:512], start=False, stop=True, r=["esel", "negmT"], w=[bk[pS]])
                    op("act", "activation", PT[ci % 3][:, c0:512], banks[pS][:, c0:512], AF.Exp, scale=ISQ,
                       r=[bk[pS]], w=["PT%d" % (ci % 3)])

                def emit_pv(ci):
                    kc = ci
                    c0 = 0 if kc <= 4 * j else 128 * (kc - 4 * j)
                    for a in range(c0 // 128, 4):
                        ob = 0 if a < 2 else 1
                        col = 129 * (a % 2)
                        op("pe", "matmul", banks[ob][:, col:col + 129], PT[ci % 3][:, 128 * a:128 * a + 128],
                           V_all[:, kc, :], start=(ci == 0 and a % 2 == 0), stop=(ci == nck - 1),
                           skip_group_check=True, r=["PT%d" % (ci % 3), "V%d" % (kc // 4), "V_all"], w=[bk[ob]])

                emit_qk(0)
                if nck > 1:
                    emit_qk(1)
                for ci in range(nck):
                    if ci + 2 < nck:
                        emit_qk(ci + 2)
                    emit_pv(ci)
                for a in range(4):
                    ob = 0 if a < 2 else 1
                    col = 129 * (a % 2)
                    op("dve", "reciprocal", rinv[:, a:a + 1], banks[ob][:, col + 128:col + 129], r=[bk[ob]], w=["rinv"])
                    op("dve", "tensor_scalar", ex[:, a, 0:128], banks[ob][:, col:col + 128], rinv[:, a:a + 1], None,
                       ALU.mult, r=[bk[ob], "rinv"], w=[exk])
                dma("sp", ex_b[512 * j:512 * j + 512, :].rearrange("(a p) c -> p a c", p=128), ex[:, :, :],
                    r=[exk], w=["ex_b"])
            cc(ex_b.ap(), ex_all.ap(), r=["ex_b"], w=["ex_all"])
            sch.flush()
        if stop == 1:
            return nc

        with ExitStack() as pb_:
            sbb = lambda n, sh, dt: pb_.enter_context(nc.sbuf_tensor(n, sh, dt))
            x1 = sbb("x1", [128, GT, D], F32)
            actT = sbb("actT", [128, 16, GB], BF16)
            hT = sbb("hT", [128, KF, GB], BF16)
            Mx = [sbb("Mx%d" % i, [128, 8, 384], BF16) for i in range(2)]
            hf = sbb("hf", [128, 8, 128], F32)
            hsq = sbb("hsq", [128, 8, 128], F32)
            mo = sbb("mo", [128, 1024], BF16)
            st4 = sbb("st4", [128, 16], F32)
            nwb = sbb("nwb", [128, 1024], F32)
            bc = [sbb("bc%d" % i, [128, D], F32) for i in range(2)]
            wo = [sbb("wo%d" % i, [128, 16, 256], BF16) for i in range(2)]
            wg = [sbb("wg%d" % i, [128, 16, 256], BF16) for i in range(2)]
            wd = [sbb("wd%d" % i, [128, 11, 512], BF16) for i in range(2)]
            sg = [sbb("sg%d" % i, [128, GB], F32) for i in range(2)]
            junk = sbb("junk", [128, D], BF16)
            ls = sbb("ls", [128, 8], F32)

            dma("sp", nwb[:, :], mlnw.ap().rearrange("a b -> (a b)").partition_broadcast(128), w=["nwb"])

            def layer_norm_inplace(i):
                xt = x1[:, i, :]
                op("dve", "reduce_sum", ls[:, 0:1], xt, AX.X, r=["x1"], w=["ls"])
                op("pool", "memset", ls[:, 1:2], 0.0, w=["ls"])
                op("act", "activation", junk[:, :], xt, AF.Square, accum_out=ls[:, 1:2], r=["x1", "ls"], w=["junk", "ls"])
                op("dve", "tensor_scalar", ls[:, 2:4], ls[:, 0:2], 1.0 / D, None, ALU.mult, r=["ls"], w=["ls"])
                op("dve", "tensor_tensor", ls[:, 4:5], ls[:, 2:3], ls[:, 2:3], ALU.mult, r=["ls"], w=["ls"])
                op("dve", "tensor_tensor", ls[:, 5:6], ls[:, 3:4], ls[:, 4:5], ALU.subtract, r=["ls"], w=["ls"])
                op("act", "activation", ls[:, 6:7], ls[:, 5:6], AF.Ln, bias=eps_t[:, 0:1], r=["ls", "eps_t"], w=["ls"])
                op("act", "activation", ls[:, 6:7], ls[:, 6:7], AF.Exp, scale=-0.5, r=["ls"], w=["ls"])
                op("dve", "scalar_tensor_tensor", ls[:, 7:8], ls[:, 2:3], -1.0, ls[:, 6:7], ALU.mult, ALU.mult,
                   r=["ls"], w=["ls"])
                op("dve", "tensor_scalar", xt, xt, ls[:, 6:7], ls[:, 7:8], ALU.mult, ALU.add,
                   r=["x1", "ls"], w=["x1"])
                op("dve", "tensor_tensor", xt, xt, bc[0][:, :], ALU.mult, r=["x1", "bc0"], w=["x1"])
                op("pool", "tensor_tensor", xt, xt, bc[1][:, :], ALU.add, r=["x1", "bc1"], w=["x1"])

            exv4 = ex_all.ap().rearrange("(r o t) c -> o t r c", r=8, o=NCORE)
            wguv = wgu_all.ap().rearrange("(k p) (f c) -> p k f c", p=128, c=256)
            for g in range(NGB):
                dma("pool", bc[0][:, :], ln1w.ap().rearrange("a b -> (a b)").partition_broadcast(128), w=["bc0"])
                dma("pool", bc[1][:, :], ln1b.ap().rearrange("a b -> (a b)").partition_broadcast(128), w=["bc1"])
                for i in range(GT):
                    ti = g * GT + i
                    mb = ti % 2
                    M = Mx[mb]
                    mk_ = "Mx%d" % mb
                    dma("sp", M[:, :, :], exv4[bass.ds(off, 1), 128 * ti:128 * ti + 128, :, :]
                        .rearrange("o t r c -> (o t) r c"), r=["ex_all"], w=[mk_])
                    dma("sp", x1[:, i, :], xs[128 * ti:128 * ti + 128, :], w=["x1"])
                    op("dve", "tensor_copy", hf[:, :, :], M[:, :, 128:256], r=[mk_], w=["hf"])
                    for h in range(4):
                        op("dve", "reduce_sum", st4[:, h:h + 1],
                           hf[:, 2 * h:2 * h + 2, :].rearrange("p r c -> p (r c)"), AX.X, r=["hf"], w=["st4"])
                    op("act", "activation", hsq[:, :, :], hf[:, :, :], AF.Square, r=["hf"], w=["hsq"])
                    for h in range(4):
                        op("dve", "reduce_sum", st4[:, 4 + h:5 + h],
                           hsq[:, 2 * h:2 * h + 2, :].rearrange("p r c -> p (r c)"), AX.X, r=["hsq"], w=["st4"])
                    op("dve", "tensor_scalar", st4[:, 0:8], st4[:, 0:8], 1.0 / 256, None, ALU.mult, r=["st4"], w=["st4"])
                    op("dve", "tensor_tensor", st4[:, 8:12], st4[:, 0:4], st4[:, 0:4], ALU.mult, r=["st4"], w=["st4"])
                    op("dve", "tensor_tensor", st4[:, 8:12], st4[:, 4:8], st4[:, 8:12], ALU.subtract,
                       r=["st4"], w=["st4"])
                    op("act", "activation", st4[:, 12:16], st4[:, 8:12], AF.Ln, bias=eps_t[:, 0:1],
                       r=["st4", "eps_t"], w=["st4"])
                    op("act", "activation", st4[:, 12:16], st4[:, 12:16], AF.Exp, scale=-0.5, r=["st4"], w=["st4"])
                    op("dve", "scalar_tensor_tensor", st4[:, 8:12], st4[:, 0:4], -1.0, st4[:, 12:16], ALU.mult,
                       ALU.mult, r=["st4"], w=["st4"])
                    for h in range(4):
                        op("dve", "tensor_scalar", hf[:, 2 * h:2 * h + 2, :], hf[:, 2 * h:2 * h + 2, :],
                           st4[:, 12 + h:13 + h], st4[:, 8 + h:9 + h], ALU.mult, ALU.add, r=["hf", "st4"], w=["hf"])
                    op("dve", "tensor_tensor", hsq[:, :, :], hf[:, :, :], M[:, :, 256:384], ALU.mult,
                       r=["hf", mk_], w=["hsq"])
                    op("pool", "tensor_tensor", mo[:, :], hsq[:, :, :].rearrange("p r c -> p (r c)"), nwb[:, :],
                       ALU.mult, r=["hsq", "nwb"], w=["mo"])
                    for q4 in range(4):
                        pb = q4 % 2
                        pT = bkb[pb]
                        for kk in range(4):
                            kf = 4 * q4 + kk
                            src = mo[:, 128 * kf:128 * kf + 128] if kf < 8 else M[:, kf - 8, 0:128]
                            op("pe", "transpose", pT[:, 128 * kk:128 * kk + 128], src, identb[:, :],
                               r=["mo", mk_, "identb"], w=[bk[pb]])
                        dst = actT[:, 4 * q4:4 * q4 + 4, 128 * i:128 * i + 128]
                        srcp = pT[:, 0:512].rearrange("p (k c) -> p k c", k=4)
                        if q4 % 2 == 0:
                            op("act", "activation", dst, srcp, AF.Copy, r=[bk[pb]], w=["actT"])
                        else:
                            op("dve", "tensor_copy", dst, srcp, r=[bk[pb]], w=["actT"])
                for ct in range(8):
                    wb = ct % 2
                    dma("sp", wo[wb][:, :, :],
                        wout_all[:, 256 * ct:256 * ct + 256].rearrange("(k p) c -> p k c", p=128),
                        r=["wout_all"], w=["wo%d" % wb])
                    for i in range(GT):
                        po = 2 + ((ct * GT + i) % 2)
                        for k in range(16):
                            op("pe", "matmul", banks[po][:, 0:256], actT[:, k, 128 * i:128 * i + 128], wo[wb][:, k, :],
                               start=(k == 0), stop=(k == 15), r=["actT", "wo%d" % wb], w=[bk[po]])
                        op("dve", "scalar_tensor_tensor", x1[:, i, 256 * ct:256 * ct + 256],
                           x1[:, i, 256 * ct:256 * ct + 256], DN_ALPHA, banks[po][:, 0:256], ALU.mult, ALU.add,
                           r=["x1", bk[po]], w=["x1"])
                for i in range(GT):
                    layer_norm_inplace(i)
                for k in range(16):
                    pb = k % 2
                    for i in range(GT):
                        op("pe", "transpose", banks[pb][:, 128 * i:128 * i + 128], x1[:, i, 128 * k:128 * k + 128],
                           identf[:, :], r=["x1", "identf"], w=[bk[pb]])
                    op("act", "activation", actT[:, k, :], banks[pb][:, 0:GB], AF.Identity, bias=adaT[:, 48 + k:49 + k],
                       scale=sc2p[:, k:k + 1], r=[bk[pb], "adaT", "sc2p"], w=["actT"])
                for f in range(KF):
                    wb = f % 2
                    dma("sp" if f % 2 == 0 else "pool", wg[wb][:, :, :], wguv[:, :, f, :], r=["wgu_all"], w=["wg%d" % wb])
                    pg = 4 + 2 * (f % 2)
                    pu = pg + 1
                    for k in range(16):
                        op("pe", "matmul", banks[pg][:, 0:GB], wg[wb][:, k, 0:128], actT[:, k, :],
                           start=(k == 0), stop=(k == 15), r=["wg%d" % wb, "actT"], w=[bk[pg]])
                    for k in range(16):
                        op("pe", "matmul", banks[pu][:, 0:GB], wg[wb][:, k, 128:256], actT[:, k, :],
                           start=(k == 0), stop=(k == 15), r=["wg%d" % wb, "actT"], w=[bk[pu]])
                    op("act", "activation", sg[wb][:, :], banks[pg][:, 0:GB], AF.Silu, r=[bk[pg]], w=["sg%d" % wb])
                    op("dve", "tensor_tensor", hT[:, f, :], sg[wb][:, :], banks[pu][:, 0:GB], ALU.mult,
                       r=["sg%d" % wb, bk[pu]], w=["hT"])
                pc = 0
                for ct in range(4):
                    for q4 in range(4):
                        wb = pc % 2
                        pc += 1
                        dma("sp", wd[wb][:, :, :],
                            wdn_all[128 * 11 * q4:128 * 11 * (q4 + 1), 512 * ct:512 * ct + 512]
                            .rearrange("(k p) c -> p k c", p=128), r=["wdn_all"], w=["wd%d" % wb])
                        for i in range(GT):
                            for kk in range(11):
                                op("pe", "matmul", banks[i][:, :], hT[:, 11 * q4 + kk, 128 * i:128 * i + 128],
                                   wd[wb][:, kk, :], start=(q4 == 0 and kk == 0), stop=(q4 == 3 and kk == 10),
                                   r=["hT", "wd%d" % wb], w=[bk[i]])
                    for i in range(GT):
                        op("dve", "scalar_tensor_tensor", x1[:, i, 512 * ct:512 * ct + 512],
                           x1[:, i, 512 * ct:512 * ct + 512], DN_ALPHA, banks[i][:, :], ALU.mult, ALU.add,
                           r=["x1", bk[i]], w=["x1"])
                dma("pool", bc[0][:, :], ln2w.ap().rearrange("a b -> (a b)").partition_broadcast(128), w=["bc0"])
                dma("pool", bc[1][:, :], ln2b.ap().rearrange("a b -> (a b)").partition_broadcast(128), w=["bc1"])
                for i in range(GT):
                    ti = g * GT + i
                    layer_norm_inplace(i)
                    dma("sp", y[128 * ti:128 * ti + 128, :], x1[:, i, :], r=["x1"], w=["y"])
            sch.flush()
    return nc


def _consts(S):
    bf = ml_dtypes.bfloat16
    idx = np.arange(128)
    c = {}
    c["c_identb"] = np.eye(128, dtype=np.float32).astype(bf)
    c["c_identf"] = np.eye(128, dtype=np.float32)
    c["c_tri"] = (idx[:, None] <= idx[None, :]).astype(np.float32)
    c["c_ones"] = np.ones((128, 128), np.float32)
    c["c_trineg"] = np.where(idx[:, None] > idx[None, :], NEG, 0.0).astype(np.float32).astype(bf)
    es = np.zeros((64, 64 * 128), np.float32)
    for n in range(64):
        es[n, 128 * n:128 * n + 128] = 1.0
    c["c_esel"] = es.astype(bf)
    rp = np.zeros((128, 32), np.float32)
    for m in range(16):
        rp[m + 16, m] = 1.0
        rp[m, m + 16] = 1.0
    c["c_ropep"] = rp
    half = 16
    inv = (np.float32(ROPE_THETA) ** (-np.arange(half, dtype=np.float32) * np.float32(2.0) / np.float32(32))).astype(np.float32)
    ang = (np.arange(S, dtype=np.float32)[:, None] * inv[None, :]).astype(np.float32)
    cs = np.cos(ang).astype(np.float32).T
    sn = np.sin(ang).astype(np.float32).T
    c["c_cos"] = np.ascontiguousarray(np.concatenate([cs, cs], 0))
    c["c_sin"] = np.ascontiguousarray(np.concatenate([-sn, sn], 0))
    return c


def make_in_maps(S, x, c, w_ada, b_ada, w_in, conv_w, conv_b, ml_igate_b, ml_fgate_b, ml_norm_w,
                 w_out, ln1_w, ln1_b, w_gu, w_down, ln2_w, ln2_b):
    TPC = S // NCORE
    f32 = np.float32
    x = np.asarray(x, f32)[0]
    w_ada = np.asarray(w_ada, f32)[0]
    b_ada = np.asarray(b_ada, f32)[0]
    w_in = np.asarray(w_in, f32)[0]
    conv_w = np.asarray(conv_w, f32)[0]
    conv_b = np.asarray(conv_b, f32)[0]
    igb = np.asarray(ml_igate_b, f32)[0]
    fgb = np.asarray(ml_fgate_b, f32)[0]
    w_out = np.asarray(w_out, f32)[0]
    w_gu = np.asarray(w_gu, f32)[0]
    w_down = np.asarray(w_down, f32)[0]
    consts = _consts(S)
    wgu_p = np.ascontiguousarray(
        w_gu.reshape(D, 2, KF, 128).transpose(0, 2, 1, 3).reshape(D, 2 * DFF))
    wdn_p = np.concatenate([w_down, np.zeros((KFP * 128 - DFF, D), f32)], 0)
    maps = []
    for cc in range(NCORE):
        h, half = cc // 2, cc % 2
        cols = np.concatenate([
            np.arange(128 * cc, 128 * cc + 128),
            1024 + np.arange(128 * cc, 128 * cc + 128),
            3072 + np.arange(128 * h, 128 * h + 128),
            3584 + np.arange(128 * h, 128 * h + 128),
            2048 + np.arange(128 * cc, 128 * cc + 128),
            4096 + 256 * h + 128 * half + np.arange(128),
            5120 + 256 * h + 128 * half + np.arange(128),
            np.array([6144 + h, 6148 + h]),
        ])
        cwq = conv_w[:, 128 * h:128 * h + 128].T
        cwk = conv_w[:, 512 + 128 * h:512 + 128 * h + 128].T
        m = {
            "xs": np.ascontiguousarray(x[cc * TPC:(cc + 1) * TPC]),
            "cvec": np.ascontiguousarray(np.asarray(c, f32).reshape(16, 128)),
            "wada": np.ascontiguousarray(w_ada[:, 1536 * cc:1536 * cc + 1536]),
            "bada": np.ascontiguousarray(b_ada[1536 * cc:1536 * cc + 1536].reshape(1, 1536)),
            "win": np.ascontiguousarray(w_in[:, cols]),
            "convw": np.ascontiguousarray(np.concatenate([cwq, cwk], 1)),
            "convb": np.ascontiguousarray(np.stack([conv_b[128 * h:128 * h + 128],
                                                    conv_b[512 + 128 * h:512 + 128 * h + 128]], 1)),
            "gateb": np.ascontiguousarray(np.broadcast_to(np.array([igb[h], fgb[h]], f32)[None, :], (128, 2))),
            "wout_s": np.ascontiguousarray(w_out[256 * cc:256 * cc + 256]),
            "wgu_s": np.ascontiguousarray(wgu_p[256 * cc:256 * cc + 256]),
            "wdn_s": np.ascontiguousarray(wdn_p[768 * cc:768 * cc + 768]),
            "ln1w": np.asarray(ln1_w, f32).reshape(1, D), "ln1b": np.asarray(ln1_b, f32).reshape(1, D),
            "ln2w": np.asarray(ln2_w, f32).reshape(1, D), "ln2b": np.asarray(ln2_b, f32).reshape(1, D),
            "mlnw": np.asarray(ml_norm_w, f32).reshape(1, 1024),
            "cidx": np.array([[cc]], np.int32),
        }
        m.update(consts)
        maps.append(m)
    return maps


_NC_CACHE = {}


def run(S, **inputs):
    if S not in _NC_CACHE:
        _NC_CACHE[S] = build(S)
    nc = _NC_CACHE[S]
    maps = make_in_maps(S, **inputs)
    res = run_bass_kernel_spmd(nc, maps, core_ids=list(range(NCORE)))
    out = np.concatenate([np.asarray(r["y"], np.float32) for r in res.results], 0)
    return out[None]


def kernel(**inputs):
    S = int(np.asarray(inputs["x"]).shape[1])
    return run(S, **inputs)
```
